# Optimizing a Trainium2 kernel written in Bass

```python
import jax, jax.numpy as jnp
from jax import lax
import numpy as np

D_MODEL = 1024
BATCH = 8
SEQ = 2048
DEPTH = 1

GRID_W = 64
CTX_LEN = 256
D_MIX = D_MODEL
D_ATTN = D_MIX // 2
D_CONV = D_MIX - D_ATTN
HEAD_DIM = 64
N_HEADS = D_ATTN // HEAD_DIM
WIN_H = 8
WIN_W = 16
CONV_K = 3
ROPE_THETA = 10000.0
RMS_EPS = 1e-6
SPLIT_POINTS = (D_ATTN, 2 * D_ATTN, 3 * D_ATTN, 4 * D_ATTN,
                4 * D_ATTN + D_CONV, 4 * D_ATTN + 2 * D_CONV, 4 * D_ATTN + 3 * D_CONV)
D_IN = 4 * D_ATTN + 4 * D_CONV

kernel_name = "hybrid_natten_shortconv_dit_block"


def rmsnorm(x, g):
    xf = x.astype(jnp.float32)
    y = xf * lax.rsqrt(jnp.mean(xf * xf, axis=-1, keepdims=True) + RMS_EPS)
    return (y * g.astype(jnp.float32)).astype(x.dtype)


def adaln(cond, w_ada, b_ada):
    m = jax.nn.silu(cond) @ w_ada + b_ada
    return jnp.split(m, 3, axis=-1)


def heads(t):
    return t.reshape(*t.shape[:-1], N_HEADS, HEAD_DIM)


def axial_rope(t, n_cols):
    S = t.shape[1]
    nf = HEAD_DIM // 4
    inv = ROPE_THETA ** (-jnp.arange(nf, dtype=jnp.float32) / nf)
    pos = jnp.arange(S, dtype=jnp.int32)
    row = (pos // n_cols).astype(jnp.float32)
    col = (pos % n_cols).astype(jnp.float32)
    ang = jnp.stack([row[:, None] * inv, col[:, None] * inv], axis=1)
    cos = jnp.cos(ang)[:, None, :, None, :]
    sin = jnp.sin(ang)[:, None, :, None, :]
    tf = t.astype(jnp.float32).reshape(*t.shape[:-1], 2, 2, nf)
    rot = jnp.stack([-tf[..., 1, :], tf[..., 0, :]], axis=-2)
    return (tf * cos + rot * sin).reshape(t.shape).astype(t.dtype)


def neighbourhood_attention(q, k, v, kc, vc, rpb):
    Bn, S, H, Dh = q.shape
    rows = S // GRID_W
    wh = min(WIN_H, rows)
    scale = Dh ** -0.5
    q_plain = q.reshape(Bn, rows, GRID_W, H, Dh)
    q_rot = axial_rope(q, GRID_W).reshape(Bn, rows, GRID_W, H, Dh)
    k_rot = axial_rope(k, GRID_W).reshape(Bn, rows, GRID_W, H, Dh)
    v_g = v.reshape(Bn, rows, GRID_W, H, Dh)
    qi = jnp.arange(rows)
    row_start = jnp.clip(qi - wh // 2, 0, rows - wh)
    row_idx = row_start[:, None] + jnp.arange(wh)[None, :]
    k_rows = k_rot[:, row_idx]
    v_rows = v_g[:, row_idx]
    cols = jnp.arange(GRID_W)
    col_start = jnp.clip(cols - WIN_W // 2, 0, GRID_W - WIN_W)
    col_valid = (cols[None, :] >= col_start[:, None]) & (cols[None, :] < col_start[:, None] + WIN_W)
    dr = row_idx - qi[:, None] + (WIN_H - 1)
    dc = jnp.clip(cols[None, :] - cols[:, None] + (WIN_W - 1), 0, 2 * WIN_W - 2)
    bias = rpb[:, dr[:, None, :, None], dc[None, :, None, :]].astype(jnp.float32)
    s_lat = jnp.einsum('biqhd,birkhd->bhiqrk', q_rot, k_rows,
                       preferred_element_type=jnp.float32) * scale + bias
    s_lat = jnp.where(col_valid[:, None, :], s_lat, -jnp.inf)
    s_ctx = jnp.einsum('biqhd,blhd->bhiql', q_plain, kc,
                       preferred_element_type=jnp.float32) * scale
    n_lat = wh * GRID_W
    s = jnp.concatenate([s_lat.reshape(*s_lat.shape[:4], n_lat), s_ctx], axis=-1)
    p = jax.nn.softmax(s, axis=-1).astype(v.dtype)
    p_lat = p[..., :n_lat].reshape(s_lat.shape)
    p_ctx = p[..., n_lat:]
    o = (jnp.einsum('bhiqrk,birkhd->biqhd', p_lat, v_rows)
         + jnp.einsum('bhiql,blhd->biqhd', p_ctx, vc))
    return o.reshape(Bn, S, H * Dh)


def context_attention(qc, kc, vc):
    Bn, L, H, Dh = qc.shape
    s = jnp.einsum('blhd,bmhd->bhlm', qc, kc, preferred_element_type=jnp.float32) * (Dh ** -0.5)
    p = jax.nn.softmax(s, axis=-1).astype(vc.dtype)
    return jnp.einsum('bhlm,bmhd->blhd', p, vc).reshape(Bn, L, H * Dh)


def centred_short_conv(u, w, b):
    L = u.shape[1]
    pad = CONV_K // 2
    up = jnp.pad(u, ((0, 0), (pad, pad), (0, 0)))
    y = b
    for i in range(CONV_K):
        y = y + up[:, i:i + L] * w[i]
    return y


def gated_short_conv(u, bg, cg, zc, conv_w, conv_b):
    return bg * centred_short_conv(cg * u, conv_w, conv_b) * jax.nn.silu(zc)


def hybrid_layer(x, ctx, c, c_ctx, w_ada, b_ada, norm_g, w_in, q_norm_g, k_norm_g,
                 rpb, conv_w, conv_b, w_out, update_ctx):
    shift, scale, gate = adaln(c, w_ada, b_ada)
    shift_c, scale_c, gate_c = adaln(c_ctx, w_ada, b_ada)
    h = rmsnorm(x, norm_g) * (1 + scale[:, None]) + shift[:, None]
    hc = rmsnorm(ctx, norm_g) * (1 + scale_c) + shift_c
    q, k, v, za, u, bg, cg, zc = jnp.split(h @ w_in, SPLIT_POINTS, axis=-1)
    q = rmsnorm(heads(q), q_norm_g)
    k = rmsnorm(heads(k), k_norm_g)
    if update_ctx:
        qc, kc, vc, zac, uc, bgc, cgc, zcc = jnp.split(hc @ w_in, SPLIT_POINTS, axis=-1)
    else:
        kc, vc = jnp.split(hc @ w_in[:, D_ATTN:3 * D_ATTN], 2, axis=-1)
    kc = rmsnorm(heads(kc), k_norm_g)
    vc = heads(vc)
    attn = neighbourhood_attention(q, k, heads(v), kc, vc, rpb) * jax.nn.silu(za)
    conv = gated_short_conv(u, bg, cg, zc, conv_w, conv_b)
    x_new = x + gate[:, None] * (jnp.concatenate([attn, conv], axis=-1) @ w_out)
    if update_ctx:
        qc = rmsnorm(heads(qc), q_norm_g)
        attn_c = context_attention(qc, kc, vc) * jax.nn.silu(zac)
        conv_c = gated_short_conv(uc, bgc, cgc, zcc, conv_w, conv_b)
        ctx_new = ctx + gate_c * (jnp.concatenate([attn_c, conv_c], axis=-1) @ w_out)
    else:
        ctx_new = ctx
    return x_new, ctx_new


def setup_inputs(seed: int = 0) -> dict:
    key = jax.random.key(seed)
    ks = jax.random.split(key, 14)
    f32 = jnp.float32
    x = jax.random.normal(ks[0], (BATCH, SEQ, D_MODEL), f32)
    c = jax.random.normal(ks[1], (BATCH, D_MODEL), f32)
    ctx = jax.random.normal(ks[2], (BATCH, CTX_LEN, D_MODEL), f32)
    c_ctx = jax.random.normal(ks[3], (D_MODEL,), f32)
    w_ada = jax.random.normal(ks[4], (DEPTH, D_MODEL, 3 * D_MODEL), f32) * (0.5 * D_MODEL ** -0.5)
    b_ada = jax.random.normal(ks[5], (DEPTH, 3 * D_MODEL), f32) * 0.01
    norm_g = 1.0 + 0.1 * jax.random.normal(ks[6], (DEPTH, D_MODEL), f32)
    w_in = jax.random.normal(ks[7], (DEPTH, D_MODEL, D_IN), f32) * D_MODEL ** -0.5
    q_norm_g = 1.0 + 0.1 * jax.random.normal(ks[8], (DEPTH, HEAD_DIM), f32)
    k_norm_g = 1.0 + 0.1 * jax.random.normal(ks[9], (DEPTH, HEAD_DIM), f32)
    rpb = 0.1 * jax.random.normal(ks[10], (DEPTH, N_HEADS, 2 * WIN_H - 1, 2 * WIN_W - 1), f32)
    conv_w = jax.random.normal(ks[11], (DEPTH, CONV_K, D_CONV), f32) * CONV_K ** -0.5
    conv_b = 0.01 * jax.random.normal(ks[12], (DEPTH, D_CONV), f32)
    w_out = jax.random.normal(ks[13], (DEPTH, D_MIX, D_MODEL), f32) * D_MIX ** -0.5
    return {"x": x, "c": c, "ctx": ctx, "c_ctx": c_ctx, "w_ada": w_ada, "b_ada": b_ada,
            "norm_g": norm_g, "w_in": w_in, "q_norm_g": q_norm_g, "k_norm_g": k_norm_g,
            "rpb": rpb, "conv_w": conv_w, "conv_b": conv_b, "w_out": w_out}


def reference(x, c, ctx, c_ctx, w_ada, b_ada, norm_g, w_in, q_norm_g, k_norm_g,
              rpb, conv_w, conv_b, w_out):
    for l in range(DEPTH):
        x, ctx = hybrid_layer(x, ctx, c, c_ctx, w_ada[l], b_ada[l], norm_g[l], w_in[l],
                              q_norm_g[l], k_norm_g[l], rpb[l], conv_w[l], conv_b[l], w_out[l],
                              update_ctx=(l < DEPTH - 1))
    return x
```

```python
import types
import numpy as np
from contextlib import ExitStack
import concourse.bass as bass
import concourse.mybir as mybir
from concourse.bass_utils import run_bass_kernel_spmd

F32 = mybir.dt.float32
BF16 = mybir.dt.bfloat16
ALU = mybir.AluOpType
AF = mybir.ActivationFunctionType

S_TOK = 2048
D = 1024
NCTX = 256
EPS = 1e-6
NEG = -30000.0


def _freeze(fn):
    if fn.__closure__ is None:
        return fn
    cells = []
    for c in fn.__closure__:
        try:
            cells.append(types.CellType(c.cell_contents))
        except ValueError:
            cells.append(c)
    return types.FunctionType(fn.__code__, fn.__globals__, fn.__name__, fn.__defaults__, tuple(cells))


def _is_psum_key(k):
    if isinstance(k, tuple):
        return k[0] in ('pj', 'axp', 'sp_', 'OT')
    return k in ('mbank', 'otb', 'denb')


class Sched:
    ENG = ['pe', 'act', 'dve', 'pool', 'sp']

    def __init__(self, nc, es):
        self.nc, self.es = nc, es
        self.sems, self.val = {}, {}
        for e in self.ENG:
            self._sem(e)
        self.seen = {e: {} for e in self.ENG}
        self.prog = {e: [] for e in self.ENG}
        self.lastw, self.readers = {}, {}
        self.nops = {e: 0 for e in self.ENG}
        self.ctx = ''
        self.trace_tags = False
        self.log = {}

    def _sem(self, name):
        if name not in self.sems:
            self.sems[name] = self.es.enter_context(self.nc.semaphore(name))
            self.val[name] = 0
        return self.sems[name]

    def _deps(self, reads, writes):
        deps = {}

        def add(tok):
            if tok is None:
                return
            s, v = tok
            if deps.get(s, 0) < v:
                deps[s] = v
        for k in reads:
            if k not in self.lastw:
                raise RuntimeError(f'read of key {k!r} before any writer was emitted (emission-order bug)')
            add(self.lastw.get(k))
            if _is_psum_key(k):
                for s, v in self.readers.get(k, {}).items():
                    add((s, v))
        for k in writes:
            add(self.lastw.get(k))
            for s, v in self.readers.get(k, {}).items():
                add((s, v))
        return deps

    def _emit_waits(self, e, deps):
        for s, v in deps.items():
            if self.seen[e].get(s, 0) >= v:
                continue
            self.seen[e][s] = v
            sem = self.sems[s]
            self.prog[e].append(lambda eng, sem=sem, v=v: eng.wait_ge(sem, v))

    def _record(self, tok, reads, writes):
        for k in writes:
            self.lastw[k] = tok
            self.readers[k] = {}
        for k in reads:
            r = self.readers.setdefault(k, {})
            if r.get(tok[0], 0) < tok[1]:
                r[tok[0]] = tok[1]

    def op(self, e, fn, reads=(), writes=(), signal=True):
        fn = _freeze(fn)
        if self.trace_tags:
            import sys as _sys
            fr = _sys._getframe(1)
            self.ctx = f'{fr.f_code.co_name}:{fr.f_lineno}'
        deps = self._deps(reads, writes)
        if e == 'pe':
            deps.pop('pe', None)
        self._emit_waits(e, deps)
        self.nops[e] += 1
        if signal:
            self.val[e] += 1
            tok = (e, self.val[e])
            self.log[tok] = self.ctx
            sem = self.sems[e]
            self.prog[e].append(lambda eng, fn=fn, sem=sem: fn(eng).then_inc(sem, 1))
        else:
            tok = (e, self.val[e] + 1)
            self.prog[e].append(lambda eng, fn=fn: fn(eng))
        self._record(tok, reads, writes)
        return tok

    def dma(self, q, out, in_, reads=(), writes=(), slot=None):
        deps = self._deps(reads, writes)
        self._emit_waits(q, deps)
        sname = 'd_' + slot
        sem = self._sem(sname)
        self.val[sname] += 16
        tok = (sname, self.val[sname])
        self.prog[q].append(lambda eng, out=out, in_=in_, sem=sem: eng.dma_start(out=out, in_=in_).then_inc(sem, 16))
        self._record(tok, reads, writes)
        return tok

    def wait_tokens(self, e, toks):
        deps = {}
        for s, v in toks:
            if deps.get(s, 0) < v:
                deps[s] = v
        self._emit_waits(e, deps)


class Ring:
    def __init__(self, name, aps):
        self.name, self.aps, self.i = name, aps, 0

    def next(self):
        j = self.i % len(self.aps)
        self.i += 1
        return self.aps[j], (self.name, j)


def row_blocks(i):
    rs = min(max(i - 4, 0), 24)
    blocks = []
    if rs % 2 == 0:
        for t in range(4):
            r = rs + 2 * t
            blocks.append([(r, 2, 0, r - i + 7)])
    else:
        ra, rb = rs + 7, rs
        blocks.append([(ra, 1, 0, ra - i + 7), (rb, 1, 64, rb - i + 7)])
        for t in range(3):
            r = rs + 1 + 2 * t
            blocks.append([(r, 2, 0, r - i + 7)])
    return blocks


def _band(i):
    rs = min(max(i - 4, 0), 24)
    return rs, rs + 7


def _variant(i):
    return 'top' if i <= 3 else ('bot' if i >= 29 else 'int')


_SLOT0 = {'int': (0, -3, 9), 'top': (9, -6, 10), 'bot': (19, -1, 9)}


def _slot(i, j):
    base, d0, n = _SLOT0[_variant(i)]
    d = i - 2 * j
    assert 0 <= d - d0 < n, (i, j)
    return base + d - d0


def tile_rows(j):
    return [i for i in range(32) if _band(i)[0] <= 2 * j + 1 and 2 * j <= _band(i)[1]]


def tg_plan(tg):
    plan = []
    for j in range(16):
        rows = [i for i in tile_rows(j) if 8 * tg <= i < 8 * tg + 8]
        if not rows:
            continue
        assert rows == list(range(rows[0], rows[0] + len(rows)))
        segs = []
        for i in rows:
            sl = _slot(i, j)
            if segs and segs[-1][3] == _variant(i) and segs[-1][2] + segs[-1][1] == sl:
                segs[-1][1] += 1
            else:
                segs.append([i, 1, sl, _variant(i)])
        plan.append((j, rows[0], len(rows), [(a, b, c_) for a, b, c_, _ in segs]))
    return plan


def build_program(debug=(), upto='all', npairs=4, CUT=None, INTERLEAVE=True, NA=1, NB=1, TAGS=False, UNIRING=True, QEV='dve', RATIO=0.7, RD=0.62, RF0=0.1, RF1=1.0, PRO_LAG=1000):
    nc = bass.Bass("TRN2", target_bir_lowering=False)
    Dm = {}

    def din(name, shape):
        Dm[name] = nc.dram_tensor(name, shape, F32, kind="ExternalInput").ap()
    din('x', [S_TOK, D]); din('ctx', [NCTX, D]); din('cvec', [128, 16]); din('w_ada', [D, 3 * D])
    din('bada', [128, 24]); din('normg', [128, 8]); din('w_in', [D, 4096]); din('w_out', [D, D])
    din('qkg', [128, 2]); din('biasBT', [8, 128, 1792]); din('convw', [128, 12]); din('convb', [128, 4])
    din('ident', [128, 128]); din('perm', [128, 128]); din('bo2', [128, 2]); din('swap', [128, 128])
    din('cos', [128, S_TOK]); din('sin', [128, S_TOK])
    out = nc.dram_tensor('out', [S_TOK, D], F32, kind="ExternalOutput").ap()
    dbg_out = {}

    with ExitStack() as es:
        S = Sched(nc, es)
        S.trace_tags = TAGS
        build_program.last_sched = S

        def sb(name, shape, dt):
            return es.enter_context(nc.sbuf_tensor('sb_' + name, shape, dt))

        def ps(name):
            return es.enter_context(nc.psum_tensor('ps_' + name, [128, 512], F32))

        ident_f = sb('ident_f', [128, 128], F32)
        ident_b = sb('ident_b', [128, 128], BF16)
        perm_b = sb('perm_b', [128, 128], BF16)
        bo2_b = sb('bo2_b', [128, 2], BF16)
        cos_b = sb('cos_b', [128, S_TOK], BF16)
        sin_b = sb('sin_b', [128, S_TOK], BF16)
        cvec = sb('cvec', [128, 16], F32)
        sc_b = sb('sc_b', [128, 16], BF16)
        bada = sb('bada', [128, 24], F32)
        normg = sb('normg', [128, 8], F32)
        qkg = sb('qkg', [128, 2], F32)
        gq_s = sb('gq_s', [128, 1], F32)
        convw = sb('convw', [128, 12], F32)
        convb = sb('convb', [128, 4], F32)
        diag = sb('diag', [128, 12, 128], BF16)
        mod = sb('mod', [128, 24, 2], F32)
        gs = sb('gs', [128, 8, 2], F32)
        gate_rep = sb('gate_rep', [128, 128], BF16)
        Gbc = sb('Gbc', [128, D], BF16)
        ssx = sb('ssx', [128, 18], F32)
        ax = sb('ax', [128, 18], F32)
        rstdx = sb('rstdx', [128, 18], F32)
        rs_t = [sb('rs_t0', [128, 72], F32)] * 2
        rs_y = [sb('rs_y0', [128, 72], F32)] * 2
        th_c = rs_t[0][:, 0:16]
        sc_f = rs_t[0][:, 16:32]
        rs_y2 = [sb('rs_y20', [128, 72], F32)] * 2
        a68_2 = [sb('a68_0', [128, 68], F32)] * 2
        rstd68_2 = [sb('rstd68_0', [128, 68], F32)] * 2
        rstd_rep4 = [sb(f'rstd_rep{i}', [128, 4, 128], BF16) for i in range(2)]
        rstd_b2 = [sb('rstd_b0', [128, 68], BF16)] * 2
        hT = sb('hT', [128, 8, S_TOK], BF16)
        hcT = sb('hcT', [128, 8, NCTX], BF16)
        yT = sb('yT', [128, 8, S_TOK], BF16)
        wp = [sb(f'wp{i}', [128, 8, 512], BF16) for i in range(2)]
        wsm = [sb(f'wsm{i}', [128, 8, 128], BF16) for i in range(5)]
        V_aug = sb('V_aug', [128, 16, 4, 3, 64], BF16)
        Vc_aug = sb('Vc_aug', [128, 2, 4, 3, 64], BF16)
        kcT2 = [sb(f'kcT{i}', [128, NCTX], BF16) for i in range(2)]
        qrot2 = [sb(f'qrot{i}', [128, S_TOK], BF16) for i in range(2)]
        qpl2 = [sb(f'qpl{i}', [128, S_TOK], BF16) for i in range(2)]
        krot2 = [sb(f'krot{i}', [128, S_TOK], BF16) for i in range(2)]
        sg2 = [sb(f'sg{i}', [128, S_TOK], BF16) for i in range(2)]
        sqr = [sb(f'sq{i}', [128, 512], BF16) for i in range(2)]
        qgr = [sb(f'qg{i}', [128, 512], BF16) for i in range(2)]
        t1r = [sb(f't1{i}', [128, 512], BF16) for i in range(1)]
        t2r = [sb(f't2{i}', [128, 512], BF16) for i in range(1)]
        thr = [sb(f'th{i}', [128, 512], BF16) for i in range(1)]
        bt2 = sb('bt2', [128, 2, 28, 64], BF16)
        Plr = [sb(f'Pl{i}', [128, 512], BF16) for i in range(3)]
        Pcr = [sb(f'Pc{i}', [128, 2, 512], BF16) for i in range(2)]
        rD = sb('rD', [128, 512], F32)
        rT = rD[:].bitcast(BF16)[:, 0:512]
        gD = sb('gD', [128, 512], BF16)
        oS = sb('oS', [128, 512], BF16)
        a_buf = sb('a_buf', [128, S_TOK + 2], BF16)
        gz = sb('gz', [128, S_TOK], BF16)
        junk = gz[:, 0:D]
        cgr = [sb(f'cg{i}', [128, 512], BF16) for i in range(1)]
        tzr = [sb(f'tz{i}', [128, 512], BF16) for i in range(1)]

        pj = Ring('pj', [ps('pj0'), ps('pj1')])
        PJ3 = False
        ax0_ps = ps('ax0')
        den_ps = ps('ax1')
        axr = Ring('axp', [ax0_ps])
        if UNIRING:
            pj.aps.append(ax0_ps)
            pj.name = 'pj'
            axr = pj
        sr = Ring('sp_', [ps('s0'), ps('s1')])
        o_ps = ps('o0')
        m_ps = ps('m0')
        if PJ3:
            pj.aps.append(m_ps)

        def f32view(t, k):
            return t[:, k, :].bitcast(F32)
        xst_ring = Ring('xst', [f32view(yT, k) for k in range(8)])
        xst_keys = [[('yT', k, tg) for tg in range(4)] for k in range(8)]

        wp_ring = Ring('wp', [w for w in wp])
        wsm_ring = Ring('wsm', wsm[0:2])
        wsmc_ring = Ring('wsmc', wsm[2:5])
        sq_ring = Ring('sq', sqr); qg_ring = Ring('qg', qgr); t1_ring = Ring('t1', t1r)
        t2_ring = Ring('t2', t2r); th_ring = Ring('th', thr); Pl_ring = Ring('Pl', Plr)
        Pc_ring = Ring('Pc', Pcr); cg_ring = Ring('cg', cgr); rep_ring = Ring('rep', rstd_rep4)
        tz_ring = Ring('tz', tzr)

        dma_ctr = [0]

        def slot():
            dma_ctr[0] += 1
            return f's{dma_ctr[0]}'

        out_toks, dbg_toks = [], []
        class _Cut(Exception):
            pass
        def cut(name):
            if CUT == name:
                raise _Cut()
        def body():
            S.dma('sp', ident_f[:], Dm['ident'][:, :], writes=['ident_f'], slot=slot())
            S.dma('sp', cvec[:], Dm['cvec'][:, :], writes=['cvec'], slot=slot())
            S.dma('sp', bada[:], Dm['bada'][:, :], writes=['bada'], slot=slot())
            S.dma('sp', normg[:], Dm['normg'][:, :], writes=['normg'], slot=slot())
            S.dma('sp', qkg[:], Dm['qkg'][:, :], writes=['qkg'], slot=slot())
            S.dma('sp', convw[:], Dm['convw'][:, :], writes=['convw'], slot=slot())
            S.dma('sp', convb[:], Dm['convb'][:, :], writes=['convb'], slot=slot())
            S.dma('pool', ident_b[:], Dm['ident'][:, :], writes=['ident_b'], slot=slot())
            S.dma('pool', bo2_b[:], Dm['bo2'][:, :], writes=['bo2_b'], slot=slot())
            S.dma('pool', perm_b[:], Dm['perm'][:, :], writes=['perm_b'], slot=slot())

            cut('loads')
            HT_ALL = [('hT', k, o) for k in range(8) for o in range(0, S_TOK, 512)]
            HC_ALL = [('hcT', k, 0) for k in range(8)]
            wada_buf = {}
            for pc in range(4):
                w = hT[:, 2 * pc:2 * pc + 2, :].rearrange("p a (b n) -> p (a b) n", b=4)
                kw = [('hT', k, o) for k in (2 * pc, 2 * pc + 1) for o in range(0, S_TOK, 512)]
                S.dma('pool', w, Dm['w_ada'][:, pc * 512:(pc + 1) * 512].rearrange("(k p) n -> p k n", p=128),
                      writes=kw, slot=f'wa{pc}')
                wada_buf[pc] = (w, kw)
            wv, kwv = wp_ring.next()
            S.dma('pool', wv[:], Dm['w_in'][:, 1024:1536].rearrange("(k p) n -> p k n", p=128), writes=[kwv], slot=f'wp{kwv[1]}')

            S.op('act', lambda e: e.activation(out=th_c, in_=cvec[:], func=AF.Tanh, scale=0.5),
                 reads=['cvec'], writes=['th_c', ('rs_t', 0)])
            S.op('dve', lambda e: e.scalar_tensor_tensor(out=sc_f, in0=th_c, scalar=1.0, in1=cvec[:],
                                                         op0=ALU.add, op1=ALU.mult), reads=['th_c', 'cvec', ('rs_t', 0)], writes=['sc_f', ('rs_t', 0)])
            S.op('dve', lambda e: e.tensor_single_scalar(out=sc_b[:], in_=sc_f, scalar=0.5, op=ALU.mult),
                 reads=['sc_f', ('rs_t', 0)], writes=['sc_b'])
            S.op('dve', lambda e: e.tensor_single_scalar(out=gq_s[:], in_=qkg[:, 0:1], scalar=0.125, op=ALU.mult),
                 reads=['qkg'], writes=['gq_s'])
            S.op('pool', lambda e: e.memset(V_aug[:, :, :, 1:2, :], 1.0), writes=['V_ones'])
            S.op('pool', lambda e: e.memset(Vc_aug[:, :, :, 1:2, :], 1.0), writes=['Vc_ones'])
            S.op('pool', lambda e: e.memset(gate_rep[:], 1.0), writes=['gate_rep'])
            S.op('pool', lambda e: e.memset(a_buf[:, 0:1], 0.0), writes=['a_pad'])
            S.op('pool', lambda e: e.memset(a_buf[:, S_TOK + 1:S_TOK + 2], 0.0), writes=['a_pad'])

            cut('misc')

            def rsqrt_small(a_ap, out_ap, n, rkeys, wkeys, iters=5, tag=0):
                t, y, y2 = rs_t[tag][:, 0:n], rs_y[tag][:, 0:n], rs_y2[tag][:, 0:n]
                kt, ky, ky2 = ('rs_t', 0), ('rs_y', 0), ('rs_y2', 0)
                S.op('dve', lambda e: e.tensor_scalar(out=t, in0=a_ap, scalar1=0.5, scalar2=0.5, op0=ALU.mult, op1=ALU.add),
                     reads=rkeys, writes=[kt])
                S.op('dve', lambda e: e.reciprocal(out=y, in_=t), reads=[kt], writes=[ky])
                E = 'dve'
                for it in range(iters):
                    last = it == iters - 1
                    S.op(E, lambda e: e.tensor_tensor(out=y2, in0=y, in1=y, op=ALU.mult), reads=[ky], writes=[ky2])
                    S.op(E, lambda e: e.scalar_tensor_tensor(out=t, in0=y2, scalar=-0.5, in1=a_ap, op0=ALU.mult, op1=ALU.mult),
                         reads=[ky2] + list(rkeys), writes=[kt])
                    S.op(E, lambda e: e.scalar_tensor_tensor(out=(out_ap if last else y), in0=t, scalar=1.5, in1=y, op0=ALU.add, op1=ALU.mult),
                         reads=[ky, kt], writes=(wkeys if last else [ky]))

            evac_flip = [0]

            def evac_affine(out_ap, in_ap, scale_ap, bias_ap, reads, writes):
                evac_flip[0] ^= 1
                if evac_flip[0]:
                    S.op('act', lambda e: e.activation(out=out_ap, in_=in_ap, func=AF.Identity, bias=bias_ap, scale=scale_ap),
                         reads=reads, writes=writes)
                else:
                    S.op('dve', lambda e: e.tensor_scalar(out=out_ap, in0=in_ap, scalar1=scale_ap, scalar2=bias_ap,
                                                          op0=ALU.mult, op1=ALU.add), reads=reads, writes=writes)

            XS = []
            for nm, bufs in (('qrot', qrot2), ('qpl', qpl2), ('krot', krot2), ('sg', sg2)):
                for pq_ in range(2):
                    XS.append((bufs[pq_][:, 0:1024], [(nm, pq_, 0), (nm, pq_, 1)]))
                    XS.append((bufs[pq_][:, 1024:2048], [(nm, pq_, 2), (nm, pq_, 3)]))
            for j_ in range(2):
                XS.append((Pcr[j_][:].rearrange("p a b -> p (a b)"), [(('Pc', j_), 0), (('Pc', j_), 1)]))

            def front(src, row0, ntile, col0, xs0):
                stg = []
                for t4 in range(ntile):
                    xst, kx = xst_ring.next()
                    keys = xst_keys[kx[1]]
                    S.dma('sp', xst, src[row0 + t4 * 128: row0 + (t4 + 1) * 128, :], writes=keys, slot=f'x{kx[1]}')
                    c_ = col0 + t4
                    S.op('act', lambda e: e.activation(out=junk, in_=xst, func=AF.Square, accum_out=ssx[:, c_:c_ + 1]),
                         reads=keys, writes=[('gz', 0), ('gz', 1), ('ssx', c_)])
                    stg.append((xst, keys))
                S.op('dve', lambda e: e.tensor_scalar(out=ax[:, col0:col0 + ntile], in0=ssx[:, col0:col0 + ntile],
                                                      scalar1=1.0 / D, scalar2=EPS, op0=ALU.mult, op1=ALU.add),
                     reads=[('ssx', col0 + t) for t in range(ntile)], writes=[('ax', col0)])
                rsqrt_small(ax[:, col0:col0 + ntile], rstdx[:, col0:col0 + ntile], ntile, [('ax', col0)], [('rstdx', col0)], iters=4, tag=(col0 // 8) % 2)
                for t4 in range(ntile):
                    xst, keys = stg[t4]
                    xs_ap, xs_keys = XS[xs0 + t4]
                    c_ = col0 + t4
                    S.op('dve', lambda e: e.tensor_scalar(out=xs_ap, in0=xst, scalar1=rstdx[:, c_:c_ + 1], scalar2=None, op0=ALU.mult),
                         reads=keys + [('rstdx', col0)], writes=xs_keys)

            front(Dm['x'], 0, 8, 0, 0)

            def adaln_piece(pc):
                w, kw = wada_buf[pc]
                for jj in range(4):
                    j = pc * 4 + jj
                    for k in range(8):
                        S.op('pe', lambda e: e.matmul(m_ps[:, j * 2:j * 2 + 2], lhsT=w[:, k, jj * 128:(jj + 1) * 128],
                                                      rhs=sc_b[:, k * 2:k * 2 + 2], start=(k == 0), stop=(k == 7)),
                             reads=list(kw) + ['sc_b'], writes=[('m_ada', j), 'mbank'], signal=(k == 7))

            for pc in range(4):
                adaln_piece(pc)
            S.op('dve', lambda e: e.tensor_tensor(out=mod[:, 0:16, :], in0=m_ps[:, 0:32].rearrange("p (j t) -> p j t", t=2),
                                                  in1=bada[:, 0:16].unsqueeze(2).to_broadcast([128, 16, 2]), op=ALU.add),
                 reads=[('m_ada', j) for j in range(16)] + ['bada', 'mbank'], writes=['mod_ss'])
            S.op('dve', lambda e: e.scalar_tensor_tensor(out=gs[:], in0=mod[:, 8:16, :], scalar=1.0,
                                                         in1=normg[:].unsqueeze(2).to_broadcast([128, 8, 2]),
                                                         op0=ALU.add, op1=ALU.mult), reads=['mod_ss', 'normg'], writes=['gs'])
            cut('ada')

            pro_ring = Ring('pro', [ax0_ps] + sr.aps[0:2] + [o_ps, den_ps])
            pro_keys = [[('pj', 2) if UNIRING else ('axp', 0)], [('sp_', 0)], [('sp_', 1)], [('OT', 0), ('OT', 1), 'otb'], [('OT', 0), ('OT', 1), 'denb']]

            def back(ntile, xs0, dst, dst_off, mcol, dname):
                for k in range(8):
                    yield
                    pst, kp_ = pro_ring.next()
                    kpl = pro_keys[kp_[1]]
                    kp = kpl[-1]
                    pb = pst[:].bitcast(BF16)
                    for t4 in range(ntile):
                        xs_ap, xs_keys = XS[xs0 + t4]
                        S.op('pe', lambda e: e.transpose(pb[:, t4 * 128:(t4 + 1) * 128], xs_ap[:, k * 128:(k + 1) * 128], ident_b[:]),
                             reads=xs_keys + ['ident_b'], writes=kpl, signal=(t4 == ntile - 1))
                    evac_affine(dst[:, k, dst_off:dst_off + ntile * 128], pb[:, 0:ntile * 128],
                                gs[:, k, mcol:mcol + 1], mod[:, k, mcol:mcol + 1],
                                reads=kpl + ['gs', 'mod_ss'], writes=[(dname, k, dst_off)])

            ORDER = ['pro', 'v', 'qk', 'za', 'attn', 'conv', 'all']
            lvl = ORDER.index(upto)
            vflip = [0]

            def v_tile(dst, tt, lhs_src, lhs_off, lkeys, wkey):
                pst, kp = pj.next()
                for k in range(8):
                    S.op('pe', lambda e: e.matmul(pst[:, 0:512], lhsT=lhs_src[:, k, lhs_off:lhs_off + 128], rhs=wv[:, k, :],
                                                  start=(k == 0), stop=(k == 7)),
                         reads=[kwv, lkeys(k)], writes=[kp], signal=(k == 7))
                vflip[0] ^= 1
                o_ap = dst[:, tt, :, 0:3:2, :]
                i_ap = pst[:, 0:512].rearrange("p (c l d) -> p c l d", c=4, l=2)
                if vflip[0]:
                    S.op('act', lambda e: e.activation(out=o_ap, in_=i_ap, func=AF.Identity), reads=[kp], writes=[wkey])
                else:
                    S.op('dve', lambda e: e.tensor_copy(out=o_ap, in_=i_ap), reads=[kp], writes=[wkey])

            def v_group(tg):
                if lvl >= 1:
                    for tt in range(tg * 4, tg * 4 + 4):
                        v_tile(V_aug, tt, hT, tt * 128, lambda k, tt=tt: ('hT', k, (tt // 4) * 512), ('V', tt))
                        yield

            def pro_gen():
                yield from back(4, 0, hT, 0, 0, 'hT')
                yield from back(4, 4, hT, 512, 0, 'hT')
                front(Dm['x'], 1024, 8, 8, 8)
                yield from v_group(0)
                front(Dm['ctx'], 0, 2, 16, 16)
                for i in range(12):
                    S.op('dve', lambda e: e.tensor_scalar(out=diag[:, i, :], in0=ident_f[:], scalar1=convw[:, i:i + 1],
                                                          scalar2=None, op0=ALU.mult),
                         reads=['ident_f', 'convw'], writes=[('diag', i)])
                yield from back(4, 8, hT, 1024, 0, 'hT')
                yield from v_group(1)
                yield from back(4, 12, hT, 1536, 0, 'hT')
                yield from v_group(2)
                yield from back(2, 16, hcT, 0, 1, 'hcT')
                yield from v_group(3)
                if lvl >= 1:
                    for tt in range(2):
                        v_tile(Vc_aug, tt, hcT, tt * 128, lambda k: ('hcT', k, 0), ('Vc', tt))
                        yield

            PRO = pro_gen()
            if not (lvl >= 5 and npairs == 4 and INTERLEAVE):
                for _ in PRO:
                    pass
            cut('norm')

            def gate_dma_stage():
                for pc in range(4, 6):
                    wt, kwt = wp_ring.next()
                    S.dma('pool', wt[:], Dm['w_ada'][:, pc * 512:(pc + 1) * 512].rearrange("(k p) n -> p k n", p=128),
                          writes=[kwt], slot=f'wp{kwt[1]}')
                    wada_buf[pc] = (wt, [kwt])
                yield

            def gate_mm_stage():
                for pc in range(4, 6):
                    adaln_piece(pc)
                    yield
                S.op('dve', lambda e: e.tensor_tensor(out=mod[:, 16:24, :], in0=m_ps[:, 32:48].rearrange("p (j t) -> p j t", t=2),
                                                      in1=bada[:, 16:24].unsqueeze(2).to_broadcast([128, 8, 2]), op=ALU.add),
                     reads=[('m_ada', j) for j in range(16, 24)] + ['bada', 'mbank'], writes=['mod_g'])
                yield
                for half in range(2):
                    for jj in range(4):
                        j = half * 4 + jj
                        S.op('dve', lambda e: e.tensor_scalar(out=rT[:, jj * 128:(jj + 1) * 128], in0=ident_f[:], scalar1=mod[:, 16 + j, 0:1],
                                                              scalar2=None, op0=ALU.mult), reads=['mod_g', 'ident_f'], writes=[('rD', 0), ('rD', 1)])
                    pst, kp = axr.next()
                    S.op('pe', lambda e: e.matmul(pst[:], lhsT=gate_rep[:], rhs=rT, start=True, stop=True),
                         reads=['gate_rep', ('rD', 0), ('rD', 1)], writes=[kp])
                    S.op('act', lambda e: e.activation(out=Gbc[:, half * 512:(half + 1) * 512], in_=pst[:], func=AF.Identity),
                         reads=[kp], writes=[('Gbc', half)])
                    yield
                for half in range(2):
                    w, kw = wp_ring.next()
                    S.dma('pool', w[:], Dm['w_out'][:, half * 512:(half + 1) * 512].rearrange("(k p) n -> p k n", p=128),
                          writes=[kw], slot=f'wp{kw[1]}')
                    wo.append((w, kw))
                yield

            def wout_fold_stage():
                for half in range(2):
                    w, kw = wo[half]
                    S.op('pool', lambda e: e.tensor_tensor(out=w[:], in0=w[:],
                                                           in1=Gbc[:, half * 512:(half + 1) * 512].unsqueeze(1).to_broadcast([128, 8, 512]),
                                                           op=ALU.mult), reads=[kw, ('Gbc', half)], writes=[kw])
                    yield

            wo = []

            cut('gbc')
            S.dma('pool', cos_b[:], Dm['cos'][:, :], writes=['cos_b'], slot=slot())
            S.dma('pool', sin_b[:], Dm['sin'][:, :], writes=['sin_b'], slot=slot())

            def load_wsm(col0, ring=None):
                w, kw = (ring or wsm_ring).next()
                S.dma('pool', w[:], Dm['w_in'][:, col0:col0 + 128].rearrange("(k p) n -> p k n", p=128),
                      writes=[kw], slot=f'{kw[0]}{kw[1]}')
                return w, kw

            def proj_fm(w, kw, tg, n=512, src=None, src_keys=None):
                pst, kp = pj.next()
                for k in range(8):
                    if src is None:
                        rhs = hT[:, k, tg * 512:tg * 512 + n]
                        rk = [('hT', k, tg * 512)]
                    else:
                        rhs = src[:, k, 0:n]
                        rk = [src_keys[k]]
                    S.op('pe', lambda e, pst=pst, w=w, k=k, rhs=rhs: e.matmul(pst[:, 0:n], lhsT=w[:, k, 0:128], rhs=rhs,
                                                                             start=(k == 0), stop=(k == 7)),
                         reads=[kw] + rk, writes=[kp], signal=(k == 7))
                return pst, kp

            SS = lambda col: m_ps[:, 64 + col:64 + col + 2]

            def qk_stage(c):
                pq = c % 2
                qrot, qpl, krot, kcT = qrot2[pq], qpl2[pq], krot2[pq], kcT2[pq]
                a68, rstd68, rstd_b = a68_2[pq], rstd68_2[pq], rstd_b2[pq]
                ka68, kr68, krb_ = ('a68', 0), ('rstd68', 0), ('rstd_b', 0)
                wq, kwq = load_wsm(c * 128)
                wk, kwk = load_wsm(512 + c * 128)
                def q_front(ti, tg):
                    w, kw, dstbuf, dkey, gap = ((wq, kwq, qrot, 'qrot', gq_s[:, 0:1]), (wk, kwk, krot, 'krot', qkg[:, 1:2]))[ti]
                    pst, kp = proj_fm(w, kw, tg)
                    sq, ksq = sq_ring.next()
                    S.op('act', lambda e: e.activation(out=sq[:], in_=pst[:], func=AF.Square), reads=[kp], writes=[ksq])
                    if ti == 0:
                        qg, kqg = qpl[:, tg * 512:(tg + 1) * 512], ('qpl', pq, tg)
                    else:
                        qg_t, kqg = qg_ring.next()
                        qg = qg_t[:]
                    if QEV == 'dve':
                        S.op('dve', lambda e: e.tensor_scalar(out=qg, in0=pst[:], scalar1=gap, scalar2=None, op0=ALU.mult),
                             reads=[kp, 'gq_s', 'qkg'], writes=[kqg])
                    else:
                        S.op('act', lambda e: e.activation(out=qg, in_=pst[:], func=AF.Identity, scale=gap),
                             reads=[kp, 'gq_s', 'qkg'], writes=[kqg])
                    return ti, tg, sq, ksq, qg, kqg, dstbuf, dkey

                def q_back(ti, tg, sq, ksq, qg, kqg, dstbuf, dkey):
                    for tb in range(4):
                        col = (ti * 16 + tg * 4 + tb) * 2
                        S.op('pe', lambda e: e.matmul(SS(col), lhsT=sq[:, tb * 128:(tb + 1) * 128], rhs=bo2_b[:], start=True, stop=True),
                             reads=[ksq, 'bo2_b'], writes=[('ss', col), 'mbank'], signal=(tb == 3))
                    rq, krq = axr.next()
                    S.op('pe', lambda e: e.matmul(rq[:], lhsT=perm_b[:], rhs=qg, start=True, stop=True),
                         reads=[kqg, 'perm_b'], writes=[krq])
                    t1, kt1 = t1_ring.next()
                    t2, kt2 = t2_ring.next()
                    S.op('dve', lambda e: e.tensor_tensor(out=t1[:], in0=qg, in1=cos_b[:, tg * 512:(tg + 1) * 512], op=ALU.mult),
                         reads=[kqg, 'cos_b'], writes=[kt1])
                    S.op('dve', lambda e: e.tensor_tensor(out=t2[:], in0=rq[:], in1=sin_b[:, tg * 512:(tg + 1) * 512], op=ALU.mult),
                         reads=[krq, 'sin_b'], writes=[kt2])
                    S.op('dve', lambda e: e.tensor_tensor(out=dstbuf[:, tg * 512:(tg + 1) * 512], in0=t1[:], in1=t2[:], op=ALU.add),
                         reads=[kt1, kt2], writes=[(dkey, pq, tg)])

                pend = None
                for ti in range(2):
                    for tg in range(4):
                        cur = q_front(ti, tg)
                        if pend is not None:
                            q_back(*pend)
                        pend = cur
                        yield
                q_back(*pend)
                pst, kp = proj_fm(wk, kwk, 0, n=NCTX, src=hcT, src_keys=[('hcT', k, 0) for k in range(8)])
                sq, ksq = sq_ring.next()
                S.op('act', lambda e: e.activation(out=sq[:, 0:NCTX], in_=pst[:, 0:NCTX], func=AF.Square), reads=[kp], writes=[ksq])
                S.op('act', lambda e: e.activation(out=kcT[:], in_=pst[:, 0:NCTX], func=AF.Identity, scale=qkg[:, 1:2]),
                     reads=[kp, 'qkg'], writes=[('kcT', pq)])
                for tb in range(2):
                    col = 64 + tb * 2
                    S.op('pe', lambda e: e.matmul(SS(col), lhsT=sq[:, tb * 128:(tb + 1) * 128], rhs=bo2_b[:], start=True, stop=True),
                         reads=[ksq, 'bo2_b'], writes=[('ss', col), 'mbank'], signal=(tb == 1))
                sskeys = [('ss', col) for col in range(0, 68, 2)]
                S.op('dve', lambda e: e.tensor_scalar(out=a68[:], in0=m_ps[:, 64:132], scalar1=1.0 / 64, scalar2=EPS,
                                                      op0=ALU.mult, op1=ALU.add), reads=sskeys + ['mbank'], writes=[ka68])
                yield
                rsqrt_small(a68[:], rstd68[:], 68, [ka68], [kr68], iters=4, tag=pq)
                for _ in range(8):
                    yield
                S.op('dve', lambda e: e.tensor_copy(out=rstd_b[:], in_=rstd68[:]), reads=[kr68], writes=[krb_])
                yield

                def make_rep(blk0, nblk):
                    rep, krep = rep_ring.next()
                    S.op('pool', lambda e: e.tensor_copy(
                        out=rep[:, 0:nblk, :].rearrange("p b (h d) -> p b h d", h=2),
                        in_=rstd_b[:, 2 * blk0:2 * (blk0 + nblk)].rearrange("p (b h) -> p b h", h=2).unsqueeze(3).to_broadcast([128, nblk, 2, 64])),
                        reads=[krb_], writes=[krep])
                    return rep, krep, nblk

                def bcast(rep, krep, nblk):
                    rb, krb = pj.next()
                    for tb in range(nblk):
                        S.op('pe', lambda e: e.matmul(rb[:, tb * 128:(tb + 1) * 128], lhsT=rep[:, tb, :], rhs=ident_b[:], start=True, stop=True),
                             reads=[krep, 'ident_b'], writes=[krb], signal=(tb == nblk - 1))
                    return rb, krb

                steps = [(ti, tg) for ti in range(2) for tg in range(4)] + [(2, 0)]
                blk_of = lambda st: (st[0] * 16 + st[1] * 4, 4) if st[0] < 2 else (32, 2)
                reps = [make_rep(*blk_of(steps[0]))]
                for n_, (ti, tg) in enumerate(steps):
                    if n_ + 1 < len(steps):
                        reps.append(make_rep(*blk_of(steps[n_ + 1])))
                    rb, krb = bcast(*reps[n_])
                    if ti < 2:
                        dstbuf, dkey = ((qrot, 'qrot'), (krot, 'krot'))[ti]
                        S.op('dve', lambda e: e.tensor_tensor(out=dstbuf[:, tg * 512:(tg + 1) * 512], in0=dstbuf[:, tg * 512:(tg + 1) * 512],
                                                              in1=rb[:], op=ALU.mult),
                             reads=[krb, (dkey, pq, tg)], writes=[(dkey, pq, tg)])
                        if ti == 0:
                            S.op('dve', lambda e: e.tensor_tensor(out=qpl[:, tg * 512:(tg + 1) * 512], in0=qpl[:, tg * 512:(tg + 1) * 512],
                                                                  in1=rb[:], op=ALU.mult),
                                 reads=[krb, ('qpl', pq, tg)], writes=[('qpl', pq, tg)])
                    else:
                        S.op('dve', lambda e: e.tensor_tensor(out=kcT[:], in0=kcT[:], in1=rb[:, 0:NCTX], op=ALU.mult),
                             reads=[krb, ('kcT', pq)], writes=[('kcT', pq)])
                    yield

            def za_stage(c):
                pq = c % 2
                sg = sg2[pq]
                w, kw = load_wsm(1536 + c * 128)
                for tg in range(4):
                    pst, kp = proj_fm(w, kw, tg)
                    th, kth = th_ring.next()
                    S.op('act', lambda e: e.activation(out=th[:], in_=pst[:], func=AF.Tanh, scale=0.5), reads=[kp], writes=[kth])
                    S.op('dve', lambda e: e.scalar_tensor_tensor(out=sg[:, tg * 512:(tg + 1) * 512], in0=th[:], scalar=1.0, in1=pst[:],
                                                                 op0=ALU.add, op1=ALU.mult), reads=[kth, kp], writes=[('sg', pq, tg)])
                    yield

            def attn_stage(c):
                pq = c % 2
                qrot, qpl, krot, kcT, sg = qrot2[pq], qpl2[pq], krot2[pq], kcT2[pq], sg2[pq]
                for hl_ in range(2):
                    S.dma('pool', bt2[:, hl_, :, :].rearrange("p b c -> p (b c)"), Dm['biasBT'][2 * c + hl_, :, :],
                          writes=[('bt', hl_)], slot=f'bt{hl_}')
                ctx_ready = {}
                items = [(tg_, hl_) for tg_ in range(4) for hl_ in range(2)]

                def emit_ctx(tg_, hl_):
                    hp_ = slice(hl_ * 64, (hl_ + 1) * 64)
                    Pc_, kPc_ = Pc_ring.next()
                    for ct in range(2):
                        sps, ksp = sr.next()
                        S.op('pe', lambda e: e.matmul(sps[:], lhsT=kcT[hp_, ct * 128:(ct + 1) * 128], rhs=qpl[hp_, tg_ * 512:(tg_ + 1) * 512],
                                                      start=True, stop=True), reads=[('kcT', pq), ('qpl', pq, tg_)], writes=[ksp])
                        S.op('act', lambda e: e.activation(out=Pc_[:, ct, :], in_=sps[:], func=AF.Exp), reads=[ksp], writes=[(kPc_, ct)])
                    ctx_ready[(tg_, hl_)] = (Pc_, kPc_)

                emit_ctx(0, 0)
                for tg in range(4):
                    plan = tg_plan(tg)
                    banks, cur, used = [], [], 0
                    for ent in plan:
                        w_ = ent[2] * 64
                        if used + w_ > 512:
                            banks.append(cur)
                            cur, used = [], 0
                        cur.append((ent, used))
                        used += w_
                    banks.append(cur)
                    for hl in range(2):
                        h = 2 * c + hl
                        hp = slice(hl * 64, (hl + 1) * 64)
                        BK = o_ps if hl == 0 else den_ps
                        kO = ('OT', hl)
                        Pc, kPc = ctx_ready.pop((tg, hl))

                        def emit_scores(bank):
                            sps, ksp = sr.next()
                            ncol = 0
                            for (j, i0, n, segs), off in bank:
                                S.op('pe', lambda e: e.matmul(sps[:, off:off + n * 64], lhsT=krot[hp, j * 128:(j + 1) * 128],
                                                              rhs=qrot[hp, i0 * 64:(i0 + n) * 64], start=True, stop=False),
                                     reads=[('krot', pq, j // 4), ('qrot', pq, tg)], writes=[ksp], signal=False)
                                for si, (r0, nr, sl) in enumerate(segs):
                                    o2 = off + (r0 - i0) * 64
                                    S.op('pe', lambda e: e.matmul(sps[:, o2:o2 + nr * 64], lhsT=ident_b[:],
                                                                  rhs=bt2[:, hl, sl:sl + nr, :].rearrange("p a b -> p (a b)"),
                                                                  start=False, stop=(si == len(segs) - 1)),
                                         reads=[('bt', hl), 'ident_b'], writes=[ksp], signal=(si == len(segs) - 1))
                                ncol = off + n * 64
                            Pl, kPl = Pl_ring.next()
                            S.op('act', lambda e: e.activation(out=Pl[:, 0:ncol], in_=sps[:, 0:ncol], func=AF.Exp), reads=[ksp], writes=[kPl])
                            return Pl, kPl

                        def emit_pv_ctx():
                            for ct in range(2):
                                S.op('pe', lambda e: e.matmul(BK[:], lhsT=Vc_aug[:, ct, c, hl:hl + 2, :].rearrange("p a b -> p (a b)"),
                                                              rhs=Pc[:, ct, :], start=(ct == 0), stop=False),
                                     reads=[(kPc, ct), ('Vc', ct), 'Vc_ones'], writes=[kO], signal=False)

                        def emit_pv(bank, Pl, kPl, lastbank):
                            for bi, ((j, i0, n, segs), off) in enumerate(bank):
                                cs_ = slice((i0 - 8 * tg) * 64, (i0 - 8 * tg + n) * 64)
                                fin = lastbank and bi == len(bank) - 1
                                S.op('pe', lambda e: e.matmul(BK[:, cs_], lhsT=V_aug[:, j, c, hl:hl + 2, :].rearrange("p a b -> p (a b)"),
                                                              rhs=Pl[:, off:off + n * 64], start=False, stop=fin),
                                     reads=[kPl, ('V', j), 'V_ones'], writes=[kO], signal=(bi == len(bank) - 1))

                        sc = [emit_scores(banks[0])]
                        if len(banks) > 1:
                            sc.append(emit_scores(banks[1]))
                        emit_pv_ctx()
                        yield
                        nxt_item = items.index((tg, hl)) + 1
                        for bi in range(len(banks)):
                            emit_pv(banks[bi], sc[bi][0], sc[bi][1], bi == len(banks) - 1)
                            if bi + 2 < len(banks):
                                sc.append(emit_scores(banks[bi + 2]))
                            if bi == min(1, len(banks) - 1) and nxt_item < len(items):
                                emit_ctx(*items[nxt_item])
                            yield
                    S.op('act', lambda e: e.activation(out=oS[0:64, :], in_=o_ps[0:64, :], func=AF.Identity), reads=[('OT', 0)], writes=[('oS', 0)])
                    S.op('act', lambda e: e.activation(out=rD[0:64, :], in_=o_ps[64:128, :], func=AF.Identity), reads=[('OT', 0)], writes=[('rD', 0)])
                    S.op('dve', lambda e: e.tensor_copy(out=rD[64:128, :], in_=den_ps[0:64, :]), reads=[('OT', 1)], writes=[('rD', 1)])
                    S.op('dve', lambda e: e.tensor_copy(out=oS[64:128, :], in_=den_ps[64:128, :]), reads=[('OT', 1)], writes=[('oS', 1)])
                    S.op('dve', lambda e: e.reciprocal(out=rD[:], in_=rD[:]), reads=[('rD', 0), ('rD', 1)], writes=[('rD', 0), ('rD', 1)])
                    S.op('dve', lambda e: e.scalar_tensor_tensor(out=gD[:], in0=rD[:], scalar=0.5, in1=sg[:, tg * 512:(tg + 1) * 512],
                                                                 op0=ALU.mult, op1=ALU.mult), reads=[('rD', 0), ('rD', 1), ('sg', pq, tg)], writes=['gD'])
                    S.op('dve', lambda e: e.tensor_tensor(out=yT[:, c, tg * 512:(tg + 1) * 512], in0=oS[:], in1=gD[:], op=ALU.mult),
                         reads=[('oS', 0), ('oS', 1), 'gD'], writes=[('yT', c, tg)])
                    attn_prog['c'], attn_prog['tg'] = c, tg
                    yield

            conv_cols = []
            for cc_ in range(4):
                conv_cols += [3072 + cc_ * 128, 2048 + cc_ * 128, 3584 + cc_ * 128, 2560 + cc_ * 128]
            conv_issued = {}

            def conv_piece(i):
                if i >= len(conv_cols):
                    return None
                if i not in conv_issued:
                    conv_issued[i] = load_wsm(conv_cols[i], wsmc_ring)
                return conv_issued[i]

            def conv_stage(cc):
                base = 4 * cc
                wcg, kwcg = conv_piece(base)
                wu, kwu = conv_piece(base + 1)
                conv_piece(base + 2)
                for tg in range(4):
                    pcg, kpcg = proj_fm(wcg, kwcg, tg)
                    cg, kcg = cg_ring.next()
                    S.op('act', lambda e: e.activation(out=cg[:], in_=pcg[:], func=AF.Identity), reads=[kpcg], writes=[kcg])
                    yield
                    pu, kpu = proj_fm(wu, kwu, tg)
                    S.op('dve', lambda e: e.tensor_tensor(out=a_buf[:, 1 + tg * 512:1 + (tg + 1) * 512], in0=pu[:], in1=cg[:], op=ALU.mult),
                         reads=[kpu, kcg], writes=[('a', tg)])
                    yield
                wzc, kwzc = conv_piece(base + 2)
                wbg, kwbg = conv_piece(base + 3)
                for tg in range(4):
                    pz, kpz = proj_fm(wzc, kwzc, tg)
                    th, kth = th_ring.next()
                    S.op('act', lambda e: e.activation(out=th[:], in_=pz[:], func=AF.Tanh, scale=0.5), reads=[kpz], writes=[kth])
                    tz, ktz = tz_ring.next()
                    S.op('dve', lambda e: e.scalar_tensor_tensor(out=tz[:], in0=th[:], scalar=1.0, in1=pz[:], op0=ALU.add, op1=ALU.mult),
                         reads=[kth, kpz], writes=[ktz])
                    yield
                    pb_, kpb = proj_fm(wbg, kwbg, tg)
                    S.op('dve', lambda e: e.scalar_tensor_tensor(out=gz[:, tg * 512:(tg + 1) * 512], in0=pb_[:], scalar=0.5, in1=tz[:],
                                                                 op0=ALU.mult, op1=ALU.mult), reads=[kpb, ktz], writes=[('gz', tg)])
                    yield
                conv_piece(base + 4)
                conv_piece(base + 5)
                for tg in range(4):
                    pc_, kpc = axr.next()
                    rk = [('a', t) for t in range(max(0, tg - 1), min(4, tg + 2))] + ['a_pad']
                    for i in range(3):
                        S.op('pe', lambda e: e.matmul(pc_[:], lhsT=diag[:, cc * 3 + i, :], rhs=a_buf[:, tg * 512 + i:tg * 512 + i + 512],
                                                      start=(i == 0), stop=(i == 2)),
                             reads=rk + [('diag', cc * 3 + i)], writes=[kpc], signal=(i == 2))
                    S.op('dve', lambda e: e.scalar_tensor_tensor(out=yT[:, 4 + cc, tg * 512:(tg + 1) * 512], in0=pc_[:],
                                                                 scalar=convb[:, cc:cc + 1], in1=gz[:, tg * 512:(tg + 1) * 512],
                                                                 op0=ALU.add, op1=ALU.mult),
                         reads=[kpc, 'convb', ('gz', tg)], writes=[('yT', 4 + cc, tg)])
                    yield

            ep_ring = Ring('ep', [f32view(hT, k) for k in range(8)])
            xbufs = {}
            epi_state = {'next_tt': 0}
            attn_prog = {'c': -1, 'tg': -1}

            def xload(tt):
                xr, kxr = ep_ring.next()
                kx = [('hT', kxr[1], o) for o in range(0, S_TOK, 512)]
                S.dma('pool', xr, Dm['x'][tt * 128:(tt + 1) * 128, :], writes=kx, slot=f'ep{kxr[1]}')
                xbufs[tt] = (xr, kxr, kx)

            def epi_stage():
                for tt in range(8):
                    xload(tt)
                yield
                for tt in range(16):
                    tg = tt // 4
                    epi_state['next_tt'] = tt
                    xr, kxr, kx = xbufs[tt]
                    for half in range(2):
                        w, kw = wo[half]
                        pst, kp = pj.next()
                        for k in range(8):
                            S.op('pe', lambda e: e.matmul(pst[:], lhsT=yT[:, k, tt * 128:(tt + 1) * 128], rhs=w[:, k, :],
                                                          start=(k == 0), stop=(k == 7)),
                                 reads=[kw, ('yT', k, tg)], writes=[kp], signal=(k == 7))
                        S.op('dve', lambda e: e.tensor_tensor(out=xr[:, half * 512:(half + 1) * 512], in0=pst[:],
                                                              in1=xr[:, half * 512:(half + 1) * 512], op=ALU.add),
                             reads=[kp] + kx, writes=kx)
                        if half == 1:
                            out_toks.append(S.dma('sp', out[tt * 128:(tt + 1) * 128, :], xr, reads=kx, slot=f'o{kxr[1]}'))
                            if tt + 8 < 16:
                                xload(tt + 8)
                        epi_state['next_tt'] = tt + (1 if half == 1 else 0)
                        yield

            def run(gen):
                for _ in gen:
                    pass

            def chain(*gens):
                for g in gens:
                    yield from g

            def interleave(ga, gb, na=1, nb=1):
                ga, gb = iter(ga), iter(gb)
                da = db = False
                while not (da and db):
                    for _ in range(na):
                        if not da:
                            try:
                                next(ga)
                            except StopIteration:
                                da = True
                    for _ in range(nb):
                        if not db:
                            try:
                                next(gb)
                            except StopIteration:
                                db = True

            if lvl >= 5 and npairs == 4 and INTERLEAVE:
                Dq = [('gate_dma', gate_dma_stage())]
                for c in range(1, 4):
                    Dq += [(f'qk{c}', qk_stage(c)), (f'za{c}', za_stage(c))]
                    if c == 1:
                        Dq.append(('gate_mm', gate_mm_stage()))
                    if c == 2:
                        Dq.append(('wout_fold', wout_fold_stage()))
                Fq = [(f'conv{c}', conv_stage(c)) for c in range(4)]
                di, fi = [0], [0]

                def step(q, idx):
                    while idx[0] < len(q):
                        try:
                            next(q[idx[0]][1])
                            return True
                        except StopIteration:
                            idx[0] += 1
                    return False

                def drain_until(label):
                    i_ = [l for l, _ in Dq].index(label)
                    while di[0] <= i_:
                        if not step(Dq, di):
                            break

                startB = chain(qk_stage(0), za_stage(0))
                bdone = [False]

                def stepB():
                    if not bdone[0]:
                        try:
                            next(startB)
                        except StopIteration:
                            bdone[0] = True
                    return not bdone[0]

                npro = 0
                for _ in PRO:
                    npro += 1
                    if npro > PRO_LAG and npro % 2 == 0:
                        step(Fq, fi)
                while stepB():
                    step(Fq, fi)
                EPI = epi_stage() if lvl >= 6 else iter(())
                epi_done = [False]

                def step_epi():
                    if epi_done[0] or fi[0] < len(Fq) or di[0] < len(Dq):
                        return False
                    if not (attn_prog['c'] == 3 and epi_state['next_tt'] // 4 <= attn_prog['tg']):
                        return False
                    try:
                        next(EPI)
                    except StopIteration:
                        epi_done[0] = True
                        return False
                    return True

                accd = accf = 0.0
                for c in range(4):
                    if c >= 1:
                        drain_until(f'za{c}')
                    rd_, rf_ = (RD, RF0) if c < 3 else (1.0, RF1)
                    for _ in attn_stage(c):
                        accd += rd_
                        accf += rf_
                        while accd >= 1.0:
                            accd -= 1.0
                            step(Dq, di)
                        while accf >= 1.0:
                            accf -= 1.0
                            if not step(Fq, fi) and c == 3:
                                step_epi()
                while step(Dq, di):
                    pass
                while step(Fq, fi):
                    pass
                for _ in EPI:
                    pass
            else:
                run(gate_dma_stage())
                run(gate_mm_stage())
                run(wout_fold_stage())
                for c in range(npairs):
                    if lvl >= 2:
                        run(qk_stage(c))
                    if lvl >= 3:
                        run(za_stage(c))
                    if lvl >= 4:
                        run(attn_stage(c))
                    if lvl >= 5:
                        run(conv_stage(c))
                if lvl >= 6:
                    run(epi_stage())

            def dump(name, ap, shape, keys, dt=BF16):
                t = nc.dram_tensor('dbg_' + name, shape, dt, kind="ExternalOutput").ap()
                dbg_out[name] = t
                return S.dma('sp', t, ap, reads=keys, slot=slot())
            if 'hT' in debug:
                dbg_toks.append(dump('hT', hT[:], [128, 8, S_TOK], HT_ALL))
                dbg_toks.append(dump('hcT', hcT[:], [128, 8, NCTX], HC_ALL))
                dbg_toks.append(dump('V', V_aug[:], [128, 16, 4, 3, 64], [('V', t) for t in range(16)] + ['V_ones']))
                dbg_toks.append(dump('qrot', qrot2[1][:], [128, S_TOK], [('qrot', 1, t) for t in range(4)]))
                dbg_toks.append(dump('qpl', qpl2[1][:], [128, S_TOK], [('qpl', 1, t) for t in range(4)]))
                dbg_toks.append(dump('krot', krot2[1][:], [128, S_TOK], [('krot', 1, t) for t in range(4)]))
                dbg_toks.append(dump('kcT', kcT2[1][:], [128, NCTX], [('kcT', 1)]))
                dbg_toks.append(dump('yT', yT[:], [128, 8, S_TOK], [('yT', k, t) for k in range(8) for t in range(4)]))
                dbg_toks.append(dump('Gbc', Gbc[:], [128, D], [('Gbc', 0), ('Gbc', 1)]))
                dbg_toks.append(dump('mod', mod[:], [128, 24, 2], ['mod_ss', 'mod_g'], dt=F32))

        try:
            body()
        except _Cut:
            pass
        S.wait_tokens('sp', out_toks + dbg_toks)

        with nc.allow_low_precision(reason="bf16 1/den before the PE half-swap"), nc.Block() as block:
            @block.tensor
            def _(eng):
                for f in S.prog['pe']:
                    f(eng)

            @block.scalar
            def _(eng):
                for f in S.prog['act']:
                    f(eng)

            @block.vector
            def _(eng):
                for f in S.prog['dve']:
                    f(eng)

            @block.gpsimd
            def _(eng):
                for f in S.prog['pool']:
                    f(eng)

            @block.sync
            def _(eng):
                for f in S.prog['sp']:
                    f(eng)
    return nc, dbg_out


def _consts():
    ident = np.eye(128, dtype=np.float32)
    perm = np.zeros((128, 128), np.float32)
    for d in range(128):
        dl = d % 64
        axis, half, f = dl // 32, (dl % 32) // 16, dl % 16
        sw = (d // 64) * 64 + axis * 32 + (1 - half) * 16 + f
        perm[sw, d] = 1.0
    swap = np.zeros((128, 128), np.float32)
    for p in range(128):
        swap[(p + 64) % 128, p] = 1.0
    bo2 = np.zeros((128, 2), np.float32)
    bo2[:64, 0] = 1.0
    bo2[64:, 1] = 1.0
    nf = 16
    inv = (np.float32(10000.0) ** (-np.arange(nf, dtype=np.float32) / np.float32(nf))).astype(np.float32)
    pos = np.arange(S_TOK, dtype=np.int32)
    row = (pos // 64).astype(np.float32)
    col = (pos % 64).astype(np.float32)
    cos = np.zeros((128, S_TOK), np.float32)
    sin = np.zeros((128, S_TOK), np.float32)
    for p in range(128):
        dl = p % 64
        axis, half, f = dl // 32, (dl % 32) // 16, dl % 16
        ang = ((row if axis == 0 else col) * inv[f]).astype(np.float32)
        cos[p] = np.cos(ang)
        sin[p] = np.sin(ang) * (-1.0 if half == 0 else 1.0)
    return ident, perm, bo2, cos, sin, swap


def _bias_table(rpb):
    qc = np.arange(64)[None, :]
    kc = np.arange(64)[:, None]
    cs = np.clip(qc - 8, 0, 48)
    valid = (kc >= cs) & (kc < cs + 16)
    dc = np.clip(kc - qc + 15, 0, 30)
    BT = np.full((8, 2, 64, 28, 64), NEG, np.float32)
    for var, (base, d0, n) in _SLOT0.items():
        for s_ in range(n):
            d = d0 + s_
            for rloc in range(2):
                dr = 7 - d + rloc
                if var == 'int':
                    inband = (d <= 4) if rloc == 0 else (d >= -2)
                else:
                    inband = True
                if not inband or dr < 0 or dr > 14:
                    continue
                vals = rpb[:, dr][:, dc]
                BT[:, rloc, :, base + s_, :] = np.where(valid[None], vals, np.float32(NEG))
    return np.ascontiguousarray(BT.reshape(8, 128, 28 * 64))


_CACHE = {}


def kernel(x, c, ctx, c_ctx, w_ada, b_ada, norm_g, w_in, q_norm_g, k_norm_g, rpb, conv_w, conv_b, w_out, _debug=(), _upto='all', _npairs=4):
    x = np.asarray(x, np.float32); c = np.asarray(c, np.float32); ctx = np.asarray(ctx, np.float32)
    key = (tuple(_debug), _upto, _npairs)
    if key not in _CACHE:
        _CACHE[key] = build_program(debug=_debug, upto=_upto, npairs=_npairs)
    nc, dbg = _CACHE[key]
    ident, perm, bo2, cos, sin, swap = _consts()
    lay = lambda v, n: np.ascontiguousarray(np.asarray(v, np.float32).reshape(n, 128).T)
    shared = {
        'w_ada': np.ascontiguousarray(np.asarray(w_ada, np.float32)[0]),
        'w_in': np.ascontiguousarray(np.asarray(w_in, np.float32)[0]),
        'w_out': np.ascontiguousarray(np.asarray(w_out, np.float32)[0]),
        'bada': lay(b_ada[0], 24), 'normg': lay(norm_g[0], 8),
        'qkg': np.ascontiguousarray(np.stack([np.tile(np.asarray(q_norm_g, np.float32)[0], 2),
                                              np.tile(np.asarray(k_norm_g, np.float32)[0], 2)], axis=1)),
        'biasBT': _bias_table(np.asarray(rpb, np.float32)[0]),
        'convw': np.ascontiguousarray(np.asarray(conv_w, np.float32)[0].reshape(3, 4, 128).transpose(2, 1, 0).reshape(128, 12)),
        'convb': lay(conv_b[0], 4),
        'ident': ident, 'perm': perm, 'bo2': bo2, 'cos': cos, 'sin': sin, 'swap': swap,
    }
    cc = lay(c_ctx, 8)
    in_maps = []
    for b in range(8):
        cv = np.stack([lay(c[b], 8), cc], axis=2).reshape(128, 16)
        m = dict(shared)
        m['x'] = np.ascontiguousarray(x[b]); m['ctx'] = np.ascontiguousarray(ctx[b]); m['cvec'] = np.ascontiguousarray(cv)
        in_maps.append(m)
    res = run_bass_kernel_spmd(nc, in_maps, core_ids=list(range(8)))
    outp = np.stack([np.asarray(r['out'], np.float32) for r in res.results], axis=0)
    if _debug:
        return outp, [r for r in res.results]
    return outp
```

```python
import types
import numpy as np
from contextlib import ExitStack
import concourse.bass as bass
import concourse.mybir as mybir
from concourse.bass_utils import run_bass_kernel_spmd

F32 = mybir.dt.float32
BF16 = mybir.dt.bfloat16
ALU = mybir.AluOpType
AF = mybir.ActivationFunctionType

S_TOK = 2048
D = 1024
NCTX = 256
EPS = 1e-6
NEG = -30000.0


def _freeze(fn):
    if fn.__closure__ is None:
        return fn
    cells = []
    for c in fn.__closure__:
        try:
            cells.append(types.CellType(c.cell_contents))
        except ValueError:
            cells.append(c)
    return types.FunctionType(fn.__code__, fn.__globals__, fn.__name__, fn.__defaults__, tuple(cells))


def _is_psum_key(k):
    if isinstance(k, tuple):
        return k[0] in ('pj', 'axp', 'sp_', 'OT')
    return k in ('mbank', 'otb', 'denb')


class Sched:
    ENG = ['pe', 'act', 'dve', 'pool', 'sp']

    def __init__(self, nc, es):
        self.nc, self.es = nc, es
        self.sems, self.val = {}, {}
        for e in self.ENG:
            self._sem(e)
        self.seen = {e: {} for e in self.ENG}
        self.prog = {e: [] for e in self.ENG}
        self.lastw, self.readers = {}, {}
        self.nops = {e: 0 for e in self.ENG}
        self.ctx = ''
        self.trace_tags = False
        self.log = {}

    def _sem(self, name):
        if name not in self.sems:
            self.sems[name] = self.es.enter_context(self.nc.semaphore(name))
            self.val[name] = 0
        return self.sems[name]

    def _deps(self, reads, writes):
        deps = {}

        def add(tok):
            if tok is None:
                return
            s, v = tok
            if deps.get(s, 0) < v:
                deps[s] = v
        for k in reads:
            if k not in self.lastw:
                raise RuntimeError(f'read of key {k!r} before any writer was emitted (emission-order bug)')
            add(self.lastw.get(k))
            if _is_psum_key(k):
                for s, v in self.readers.get(k, {}).items():
                    add((s, v))
        for k in writes:
            add(self.lastw.get(k))
            for s, v in self.readers.get(k, {}).items():
                add((s, v))
        return deps

    def _emit_waits(self, e, deps):
        for s, v in deps.items():
            if self.seen[e].get(s, 0) >= v:
                continue
            self.seen[e][s] = v
            sem = self.sems[s]
            self.prog[e].append(lambda eng, sem=sem, v=v: eng.wait_ge(sem, v))

    def _record(self, tok, reads, writes):
        for k in writes:
            self.lastw[k] = tok
            self.readers[k] = {}
        for k in reads:
            r = self.readers.setdefault(k, {})
            if r.get(tok[0], 0) < tok[1]:
                r[tok[0]] = tok[1]

    def op(self, e, fn, reads=(), writes=(), signal=True):
        fn = _freeze(fn)
        if self.trace_tags:
            import sys as _sys
            fr = _sys._getframe(1)
            self.ctx = f'{fr.f_code.co_name}:{fr.f_lineno}'
        deps = self._deps(reads, writes)
        if e == 'pe':
            deps.pop('pe', None)
        self._emit_waits(e, deps)
        self.nops[e] += 1
        if signal:
            self.val[e] += 1
            tok = (e, self.val[e])
            self.log[tok] = self.ctx
            sem = self.sems[e]
            self.prog[e].append(lambda eng, fn=fn, sem=sem: fn(eng).then_inc(sem, 1))
        else:
            tok = (e, self.val[e] + 1)
            self.prog[e].append(lambda eng, fn=fn: fn(eng))
        self._record(tok, reads, writes)
        return tok

    def dma(self, q, out, in_, reads=(), writes=(), slot=None):
        deps = self._deps(reads, writes)
        self._emit_waits(q, deps)
        sname = 'd_' + slot
        sem = self._sem(sname)
        self.val[sname] += 16
        tok = (sname, self.val[sname])
        self.prog[q].append(lambda eng, out=out, in_=in_, sem=sem: eng.dma_start(out=out, in_=in_).then_inc(sem, 16))
        self._record(tok, reads, writes)
        return tok

    def wait_tokens(self, e, toks):
        deps = {}
        for s, v in toks:
            if deps.get(s, 0) < v:
                deps[s] = v
        self._emit_waits(e, deps)


class Ring:
    def __init__(self, name, aps):
        self.name, self.aps, self.i = name, aps, 0

    def next(self):
        j = self.i % len(self.aps)
        self.i += 1
        return self.aps[j], (self.name, j)


def row_blocks(i):
    rs = min(max(i - 4, 0), 24)
    blocks = []
    if rs % 2 == 0:
        for t in range(4):
            r = rs + 2 * t
            blocks.append([(r, 2, 0, r - i + 7)])
    else:
        ra, rb = rs + 7, rs
        blocks.append([(ra, 1, 0, ra - i + 7), (rb, 1, 64, rb - i + 7)])
        for t in range(3):
            r = rs + 1 + 2 * t
            blocks.append([(r, 2, 0, r - i + 7)])
    return blocks


def _band(i):
    rs = min(max(i - 4, 0), 24)
    return rs, rs + 7


def _variant(i):
    return 'top' if i <= 3 else ('bot' if i >= 29 else 'int')


_SLOT0 = {'int': (0, -3, 9), 'top': (9, -6, 10), 'bot': (19, -1, 9)}


def _slot(i, j):
    base, d0, n = _SLOT0[_variant(i)]
    d = i - 2 * j
    assert 0 <= d - d0 < n, (i, j)
    return base + d - d0


def tile_rows(j):
    return [i for i in range(32) if _band(i)[0] <= 2 * j + 1 and 2 * j <= _band(i)[1]]


def tg_plan(tg):
    plan = []
    for j in range(16):
        rows = [i for i in tile_rows(j) if 8 * tg <= i < 8 * tg + 8]
        if not rows:
            continue
        assert rows == list(range(rows[0], rows[0] + len(rows)))
        segs = []
        for i in rows:
            sl = _slot(i, j)
            if segs and segs[-1][3] == _variant(i) and segs[-1][2] + segs[-1][1] == sl:
                segs[-1][1] += 1
            else:
                segs.append([i, 1, sl, _variant(i)])
        plan.append((j, rows[0], len(rows), [(a, b, c_) for a, b, c_, _ in segs]))
    return plan


def build_program(debug=(), upto='all', npairs=4, CUT=None, INTERLEAVE=True, NA=1, NB=1, TAGS=False, UNIRING=True, QEV='act', RATIO=0.7, RD=0.58, RF0=0.1, RF1=1.0, PRO_LAG=1000):
    nc = bass.Bass("TRN2", target_bir_lowering=False)
    Dm = {}

    def din(name, shape):
        Dm[name] = nc.dram_tensor(name, shape, F32, kind="ExternalInput").ap()
    din('x', [S_TOK, D]); din('ctx', [NCTX, D]); din('cvec', [128, 16]); din('w_ada', [D, 3 * D])
    din('bada', [128, 24]); din('normg', [128, 8]); din('w_in', [D, 4096]); din('w_out', [D, D])
    din('qkg', [128, 2]); din('biasBT', [8, 128, 1792]); din('convw', [128, 12]); din('convb', [128, 4])
    din('ident', [128, 128]); din('perm', [128, 128]); din('bo2', [128, 2]); din('swap', [128, 128])
    din('cos', [128, S_TOK]); din('sin', [128, S_TOK])
    out = nc.dram_tensor('out', [S_TOK, D], F32, kind="ExternalOutput").ap()
    dbg_out = {}

    with ExitStack() as es:
        S = Sched(nc, es)
        S.trace_tags = TAGS
        build_program.last_sched = S

        def sb(name, shape, dt):
            return es.enter_context(nc.sbuf_tensor('sb_' + name, shape, dt))

        def ps(name):
            return es.enter_context(nc.psum_tensor('ps_' + name, [128, 512], F32))

        ident_f = sb('ident_f', [128, 128], F32)
        ident_b = sb('ident_b', [128, 128], BF16)
        perm_b = sb('perm_b', [128, 128], BF16)
        bo2_b = sb('bo2_b', [128, 2], BF16)
        cos_b = sb('cos_b', [128, S_TOK], BF16)
        sin_b = sb('sin_b', [128, S_TOK], BF16)
        cvec = sb('cvec', [128, 16], F32)
        sc_b = sb('sc_b', [128, 16], BF16)
        bada = sb('bada', [128, 24], F32)
        normg = sb('normg', [128, 8], F32)
        qkg = sb('qkg', [128, 2], F32)
        gq_s = sb('gq_s', [128, 1], F32)
        convw = sb('convw', [128, 12], F32)
        convb = sb('convb', [128, 4], F32)
        diag = sb('diag', [128, 12, 128], BF16)
        mod = sb('mod', [128, 24, 2], F32)
        gs = sb('gs', [128, 8, 2], F32)
        gate_rep = sb('gate_rep', [128, 128], BF16)
        Gbc = sb('Gbc', [128, D], BF16)
        ssx = sb('ssx', [128, 18], F32)
        ax = sb('ax', [128, 18], F32)
        rstdx = sb('rstdx', [128, 18], F32)
        rs_t = [sb('rs_t0', [128, 72], F32)] * 2
        rs_y = [sb('rs_y0', [128, 72], F32)] * 2
        th_c = rs_t[0][:, 0:16]
        sc_f = rs_t[0][:, 16:32]
        rs_y2 = [sb('rs_y20', [128, 72], F32)] * 2
        a68_2 = [sb('a68_0', [128, 68], F32)] * 2
        rstd68_2 = [sb('rstd68_0', [128, 68], F32)] * 2
        rstd_rep4 = [sb(f'rstd_rep{i}', [128, 4, 128], BF16) for i in range(2)]
        rstd_b2 = [sb('rstd_b0', [128, 68], BF16)] * 2
        hT = sb('hT', [128, 8, S_TOK], BF16)
        hcT = sb('hcT', [128, 8, NCTX], BF16)
        yT = sb('yT', [128, 8, S_TOK], BF16)
        wp = [sb(f'wp{i}', [128, 8, 512], BF16) for i in range(2)]
        wsm = [sb(f'wsm{i}', [128, 8, 128], BF16) for i in range(5)]
        V_aug = sb('V_aug', [128, 16, 4, 3, 64], BF16)
        Vc_aug = sb('Vc_aug', [128, 2, 4, 3, 64], BF16)
        kcT2 = [sb(f'kcT{i}', [128, NCTX], BF16) for i in range(2)]
        qrot2 = [sb(f'qrot{i}', [128, S_TOK], BF16) for i in range(2)]
        qpl2 = [sb(f'qpl{i}', [128, S_TOK], BF16) for i in range(2)]
        krot2 = [sb(f'krot{i}', [128, S_TOK], BF16) for i in range(2)]
        sg2 = [sb(f'sg{i}', [128, S_TOK], BF16) for i in range(2)]
        sqr = [sb(f'sq{i}', [128, 512], BF16) for i in range(2)]
        qgr = [sb(f'qg{i}', [128, 512], BF16) for i in range(2)]
        t1r = [sb(f't1{i}', [128, 512], BF16) for i in range(1)]
        t2r = [sb(f't2{i}', [128, 512], BF16) for i in range(1)]
        thr = [sb(f'th{i}', [128, 512], BF16) for i in range(1)]
        bt2 = sb('bt2', [128, 2, 28, 64], BF16)
        Plr = [sb(f'Pl{i}', [128, 512], BF16) for i in range(3)]
        Pcr = [sb(f'Pc{i}', [128, 2, 512], BF16) for i in range(2)]
        rD = sb('rD', [128, 512], F32)
        rT = rD[:].bitcast(BF16)[:, 0:512]
        gD = sb('gD', [128, 512], BF16)
        oS = sb('oS', [128, 512], BF16)
        a_buf = sb('a_buf', [128, S_TOK + 2], BF16)
        gz = sb('gz', [128, S_TOK], BF16)
        junk = gz[:, 0:D]
        cgr = [sb(f'cg{i}', [128, 512], BF16) for i in range(1)]
        tzr = [sb(f'tz{i}', [128, 512], BF16) for i in range(1)]

        pj = Ring('pj', [ps('pj0'), ps('pj1')])
        PJ3 = False
        ax0_ps = ps('ax0')
        den_ps = ps('ax1')
        axr = Ring('axp', [ax0_ps])
        if UNIRING:
            pj.aps.append(ax0_ps)
            pj.name = 'pj'
            axr = pj
        sr = Ring('sp_', [ps('s0'), ps('s1')])
        o_ps = ps('o0')
        m_ps = ps('m0')
        if PJ3:
            pj.aps.append(m_ps)

        def f32view(t, k):
            return t[:, k, :].bitcast(F32)
        xst_ring = Ring('xst', [f32view(yT, k) for k in range(8)])
        xst_keys = [[('yT', k, tg) for tg in range(4)] for k in range(8)]

        wp_ring = Ring('wp', [w for w in wp])
        wsm_ring = Ring('wsm', wsm[0:2])
        wsmc_ring = Ring('wsmc', wsm[2:5])
        sq_ring = Ring('sq', sqr); qg_ring = Ring('qg', qgr); t1_ring = Ring('t1', t1r)
        t2_ring = Ring('t2', t2r); th_ring = Ring('th', thr); Pl_ring = Ring('Pl', Plr)
        Pc_ring = Ring('Pc', Pcr); cg_ring = Ring('cg', cgr); rep_ring = Ring('rep', rstd_rep4)
        tz_ring = Ring('tz', tzr)

        dma_ctr = [0]

        def slot():
            dma_ctr[0] += 1
            return f's{dma_ctr[0]}'

        out_toks, dbg_toks = [], []
        class _Cut(Exception):
            pass
        def cut(name):
            if CUT == name:
                raise _Cut()
        def body():
            S.dma('sp', ident_f[:], Dm['ident'][:, :], writes=['ident_f'], slot=slot())
            S.dma('sp', cvec[:], Dm['cvec'][:, :], writes=['cvec'], slot=slot())
            S.dma('sp', bada[:], Dm['bada'][:, :], writes=['bada'], slot=slot())
            S.dma('sp', normg[:], Dm['normg'][:, :], writes=['normg'], slot=slot())
            S.dma('sp', qkg[:], Dm['qkg'][:, :], writes=['qkg'], slot=slot())
            S.dma('sp', convw[:], Dm['convw'][:, :], writes=['convw'], slot=slot())
            S.dma('sp', convb[:], Dm['convb'][:, :], writes=['convb'], slot=slot())
            S.dma('pool', ident_b[:], Dm['ident'][:, :], writes=['ident_b'], slot=slot())
            S.dma('pool', bo2_b[:], Dm['bo2'][:, :], writes=['bo2_b'], slot=slot())
            S.dma('pool', perm_b[:], Dm['perm'][:, :], writes=['perm_b'], slot=slot())

            cut('loads')
            HT_ALL = [('hT', k, o) for k in range(8) for o in range(0, S_TOK, 512)]
            HC_ALL = [('hcT', k, 0) for k in range(8)]
            wada_buf = {}
            for pc in range(4):
                w = hT[:, 2 * pc:2 * pc + 2, :].rearrange("p a (b n) -> p (a b) n", b=4)
                kw = [('hT', k, o) for k in (2 * pc, 2 * pc + 1) for o in range(0, S_TOK, 512)]
                S.dma('pool', w, Dm['w_ada'][:, pc * 512:(pc + 1) * 512].rearrange("(k p) n -> p k n", p=128),
                      writes=kw, slot=f'wa{pc}')
                wada_buf[pc] = (w, kw)
            wv, kwv = wp_ring.next()
            S.dma('pool', wv[:], Dm['w_in'][:, 1024:1536].rearrange("(k p) n -> p k n", p=128), writes=[kwv], slot=f'wp{kwv[1]}')

            S.op('act', lambda e: e.activation(out=th_c, in_=cvec[:], func=AF.Tanh, scale=0.5),
                 reads=['cvec'], writes=['th_c', ('rs_t', 0)])
            S.op('dve', lambda e: e.scalar_tensor_tensor(out=sc_f, in0=th_c, scalar=1.0, in1=cvec[:],
                                                         op0=ALU.add, op1=ALU.mult), reads=['th_c', 'cvec', ('rs_t', 0)], writes=['sc_f', ('rs_t', 0)])
            S.op('dve', lambda e: e.tensor_single_scalar(out=sc_b[:], in_=sc_f, scalar=0.5, op=ALU.mult),
                 reads=['sc_f', ('rs_t', 0)], writes=['sc_b'])
            S.op('dve', lambda e: e.tensor_single_scalar(out=gq_s[:], in_=qkg[:, 0:1], scalar=0.125, op=ALU.mult),
                 reads=['qkg'], writes=['gq_s'])
            S.op('pool', lambda e: e.memset(V_aug[:, :, :, 1:2, :], 1.0), writes=['V_ones'])
            S.op('pool', lambda e: e.memset(Vc_aug[:, :, :, 1:2, :], 1.0), writes=['Vc_ones'])
            S.op('pool', lambda e: e.memset(gate_rep[:], 1.0), writes=['gate_rep'])
            S.op('pool', lambda e: e.memset(a_buf[:, 0:1], 0.0), writes=['a_pad'])
            S.op('pool', lambda e: e.memset(a_buf[:, S_TOK + 1:S_TOK + 2], 0.0), writes=['a_pad'])

            cut('misc')

            def rsqrt_small(a_ap, out_ap, n, rkeys, wkeys, iters=5, tag=0):
                t, y, y2 = rs_t[tag][:, 0:n], rs_y[tag][:, 0:n], rs_y2[tag][:, 0:n]
                kt, ky, ky2 = ('rs_t', 0), ('rs_y', 0), ('rs_y2', 0)
                S.op('dve', lambda e: e.tensor_scalar(out=t, in0=a_ap, scalar1=0.5, scalar2=0.5, op0=ALU.mult, op1=ALU.add),
                     reads=rkeys, writes=[kt])
                S.op('dve', lambda e: e.reciprocal(out=y, in_=t), reads=[kt], writes=[ky])
                E = 'dve'
                for it in range(iters):
                    last = it == iters - 1
                    S.op(E, lambda e: e.tensor_tensor(out=y2, in0=y, in1=y, op=ALU.mult), reads=[ky], writes=[ky2])
                    S.op(E, lambda e: e.scalar_tensor_tensor(out=t, in0=y2, scalar=-0.5, in1=a_ap, op0=ALU.mult, op1=ALU.mult),
                         reads=[ky2] + list(rkeys), writes=[kt])
                    S.op(E, lambda e: e.scalar_tensor_tensor(out=(out_ap if last else y), in0=t, scalar=1.5, in1=y, op0=ALU.add, op1=ALU.mult),
                         reads=[ky, kt], writes=(wkeys if last else [ky]))

            evac_flip = [0]

            def evac_affine(out_ap, in_ap, scale_ap, bias_ap, reads, writes):
                evac_flip[0] ^= 1
                if evac_flip[0]:
                    S.op('act', lambda e: e.activation(out=out_ap, in_=in_ap, func=AF.Identity, bias=bias_ap, scale=scale_ap),
                         reads=reads, writes=writes)
                else:
                    S.op('dve', lambda e: e.tensor_scalar(out=out_ap, in0=in_ap, scalar1=scale_ap, scalar2=bias_ap,
                                                          op0=ALU.mult, op1=ALU.add), reads=reads, writes=writes)

            XS = []
            for nm, bufs in (('qrot', qrot2), ('qpl', qpl2), ('krot', krot2), ('sg', sg2)):
                for pq_ in range(2):
                    XS.append((bufs[pq_][:, 0:1024], [(nm, pq_, 0), (nm, pq_, 1)]))
                    XS.append((bufs[pq_][:, 1024:2048], [(nm, pq_, 2), (nm, pq_, 3)]))
            for j_ in range(2):
                XS.append((Pcr[j_][:].rearrange("p a b -> p (a b)"), [(('Pc', j_), 0), (('Pc', j_), 1)]))

            def front(src, row0, ntile, col0, xs0):
                stg = []
                for t4 in range(ntile):
                    xst, kx = xst_ring.next()
                    keys = xst_keys[kx[1]]
                    S.dma('sp', xst, src[row0 + t4 * 128: row0 + (t4 + 1) * 128, :], writes=keys, slot=f'x{kx[1]}')
                    c_ = col0 + t4
                    S.op('act', lambda e: e.activation(out=junk, in_=xst, func=AF.Square, accum_out=ssx[:, c_:c_ + 1]),
                         reads=keys, writes=[('gz', 0), ('gz', 1), ('ssx', c_)])
                    stg.append((xst, keys))
                S.op('dve', lambda e: e.tensor_scalar(out=ax[:, col0:col0 + ntile], in0=ssx[:, col0:col0 + ntile],
                                                      scalar1=1.0 / D, scalar2=EPS, op0=ALU.mult, op1=ALU.add),
                     reads=[('ssx', col0 + t) for t in range(ntile)], writes=[('ax', col0)])
                rsqrt_small(ax[:, col0:col0 + ntile], rstdx[:, col0:col0 + ntile], ntile, [('ax', col0)], [('rstdx', col0)], iters=4, tag=(col0 // 8) % 2)
                for t4 in range(ntile):
                    xst, keys = stg[t4]
                    xs_ap, xs_keys = XS[xs0 + t4]
                    c_ = col0 + t4
                    S.op('dve', lambda e: e.tensor_scalar(out=xs_ap, in0=xst, scalar1=rstdx[:, c_:c_ + 1], scalar2=None, op0=ALU.mult),
                         reads=keys + [('rstdx', col0)], writes=xs_keys)

            front(Dm['x'], 0, 8, 0, 0)

            def adaln_piece(pc):
                w, kw = wada_buf[pc]
                for jj in range(4):
                    j = pc * 4 + jj
                    for k in range(8):
                        S.op('pe', lambda e: e.matmul(m_ps[:, j * 2:j * 2 + 2], lhsT=w[:, k, jj * 128:(jj + 1) * 128],
                                                      rhs=sc_b[:, k * 2:k * 2 + 2], start=(k == 0), stop=(k == 7)),
                             reads=list(kw) + ['sc_b'], writes=[('m_ada', j), 'mbank'], signal=(k == 7))

            for pc in range(4):
                adaln_piece(pc)
            S.op('dve', lambda e: e.tensor_tensor(out=mod[:, 0:16, :], in0=m_ps[:, 0:32].rearrange("p (j t) -> p j t", t=2),
                                                  in1=bada[:, 0:16].unsqueeze(2).to_broadcast([128, 16, 2]), op=ALU.add),
                 reads=[('m_ada', j) for j in range(16)] + ['bada', 'mbank'], writes=['mod_ss'])
            S.op('dve', lambda e: e.scalar_tensor_tensor(out=gs[:], in0=mod[:, 8:16, :], scalar=1.0,
                                                         in1=normg[:].unsqueeze(2).to_broadcast([128, 8, 2]),
                                                         op0=ALU.add, op1=ALU.mult), reads=['mod_ss', 'normg'], writes=['gs'])
            cut('ada')

            pro_ring = Ring('pro', [ax0_ps] + sr.aps[0:2] + [o_ps, den_ps])
            pro_keys = [[('pj', 2) if UNIRING else ('axp', 0)], [('sp_', 0)], [('sp_', 1)], [('OT', 0), ('OT', 1), 'otb'], [('OT', 0), ('OT', 1), 'denb']]

            def back(ntile, xs0, dst, dst_off, mcol, dname):
                for k in range(8):
                    yield
                    pst, kp_ = pro_ring.next()
                    kpl = pro_keys[kp_[1]]
                    kp = kpl[-1]
                    pb = pst[:].bitcast(BF16)
                    for t4 in range(ntile):
                        xs_ap, xs_keys = XS[xs0 + t4]
                        S.op('pe', lambda e: e.transpose(pb[:, t4 * 128:(t4 + 1) * 128], xs_ap[:, k * 128:(k + 1) * 128], ident_b[:]),
                             reads=xs_keys + ['ident_b'], writes=kpl, signal=(t4 == ntile - 1))
                    evac_affine(dst[:, k, dst_off:dst_off + ntile * 128], pb[:, 0:ntile * 128],
                                gs[:, k, mcol:mcol + 1], mod[:, k, mcol:mcol + 1],
                                reads=kpl + ['gs', 'mod_ss'], writes=[(dname, k, dst_off)])

            ORDER = ['pro', 'v', 'qk', 'za', 'attn', 'conv', 'all']
            lvl = ORDER.index(upto)
            vflip = [0]

            def v_tile(dst, tt, lhs_src, lhs_off, lkeys, wkey):
                pst, kp = pj.next()
                for k in range(8):
                    S.op('pe', lambda e: e.matmul(pst[:, 0:512], lhsT=lhs_src[:, k, lhs_off:lhs_off + 128], rhs=wv[:, k, :],
                                                  start=(k == 0), stop=(k == 7)),
                         reads=[kwv, lkeys(k)], writes=[kp], signal=(k == 7))
                vflip[0] ^= 1
                o_ap = dst[:, tt, :, 0:3:2, :]
                i_ap = pst[:, 0:512].rearrange("p (c l d) -> p c l d", c=4, l=2)
                if vflip[0]:
                    S.op('act', lambda e: e.activation(out=o_ap, in_=i_ap, func=AF.Identity), reads=[kp], writes=[wkey])
                else:
                    S.op('dve', lambda e: e.tensor_copy(out=o_ap, in_=i_ap), reads=[kp], writes=[wkey])

            def v_group(tg):
                if lvl >= 1:
                    for tt in range(tg * 4, tg * 4 + 4):
                        v_tile(V_aug, tt, hT, tt * 128, lambda k, tt=tt: ('hT', k, (tt // 4) * 512), ('V', tt))
                        yield

            def pro_gen():
                yield from back(4, 0, hT, 0, 0, 'hT')
                yield from back(4, 4, hT, 512, 0, 'hT')
                front(Dm['x'], 1024, 8, 8, 8)
                yield from v_group(0)
                front(Dm['ctx'], 0, 2, 16, 16)
                for i in range(12):
                    S.op('dve', lambda e: e.tensor_scalar(out=diag[:, i, :], in0=ident_f[:], scalar1=convw[:, i:i + 1],
                                                          scalar2=None, op0=ALU.mult),
                         reads=['ident_f', 'convw'], writes=[('diag', i)])
                yield from back(4, 8, hT, 1024, 0, 'hT')
                yield from v_group(1)
                yield from back(4, 12, hT, 1536, 0, 'hT')
                yield from v_group(2)
                yield from back(2, 16, hcT, 0, 1, 'hcT')
                yield from v_group(3)
                if lvl >= 1:
                    for tt in range(2):
                        v_tile(Vc_aug, tt, hcT, tt * 128, lambda k: ('hcT', k, 0), ('Vc', tt))
                        yield

            PRO = pro_gen()
            if not (lvl >= 5 and npairs == 4 and INTERLEAVE):
                for _ in PRO:
                    pass
            cut('norm')

            def gate_dma_stage():
                for pc in range(4, 6):
                    wt, kwt = wp_ring.next()
                    S.dma('pool', wt[:], Dm['w_ada'][:, pc * 512:(pc + 1) * 512].rearrange("(k p) n -> p k n", p=128),
                          writes=[kwt], slot=f'wp{kwt[1]}')
                    wada_buf[pc] = (wt, [kwt])
                yield

            def gate_mm_stage():
                for pc in range(4, 6):
                    adaln_piece(pc)
                    yield
                S.op('dve', lambda e: e.tensor_tensor(out=mod[:, 16:24, :], in0=m_ps[:, 32:48].rearrange("p (j t) -> p j t", t=2),
                                                      in1=bada[:, 16:24].unsqueeze(2).to_broadcast([128, 8, 2]), op=ALU.add),
                     reads=[('m_ada', j) for j in range(16, 24)] + ['bada', 'mbank'], writes=['mod_g'])
                yield
                for half in range(2):
                    for jj in range(4):
                        j = half * 4 + jj
                        S.op('dve', lambda e: e.tensor_scalar(out=rT[:, jj * 128:(jj + 1) * 128], in0=ident_f[:], scalar1=mod[:, 16 + j, 0:1],
                                                              scalar2=None, op0=ALU.mult), reads=['mod_g', 'ident_f'], writes=[('rD', 0), ('rD', 1)])
                    pst, kp = axr.next()
                    S.op('pe', lambda e: e.matmul(pst[:], lhsT=gate_rep[:], rhs=rT, start=True, stop=True),
                         reads=['gate_rep', ('rD', 0), ('rD', 1)], writes=[kp])
                    S.op('act', lambda e: e.activation(out=Gbc[:, half * 512:(half + 1) * 512], in_=pst[:], func=AF.Identity),
                         reads=[kp], writes=[('Gbc', half)])
                    yield
                for half in range(2):
                    w, kw = wp_ring.next()
                    S.dma('pool', w[:], Dm['w_out'][:, half * 512:(half + 1) * 512].rearrange("(k p) n -> p k n", p=128),
                          writes=[kw], slot=f'wp{kw[1]}')
                    wo.append((w, kw))
                yield

            def wout_fold_stage():
                for half in range(2):
                    w, kw = wo[half]
                    S.op('pool', lambda e: e.tensor_tensor(out=w[:], in0=w[:],
                                                           in1=Gbc[:, half * 512:(half + 1) * 512].unsqueeze(1).to_broadcast([128, 8, 512]),
                                                           op=ALU.mult), reads=[kw, ('Gbc', half)], writes=[kw])
                    yield

            wo = []

            cut('gbc')
            S.dma('pool', cos_b[:], Dm['cos'][:, :], writes=['cos_b'], slot=slot())
            S.dma('pool', sin_b[:], Dm['sin'][:, :], writes=['sin_b'], slot=slot())

            def load_wsm(col0, ring=None):
                w, kw = (ring or wsm_ring).next()
                S.dma('pool', w[:], Dm['w_in'][:, col0:col0 + 128].rearrange("(k p) n -> p k n", p=128),
                      writes=[kw], slot=f'{kw[0]}{kw[1]}')
                return w, kw

            def proj_fm(w, kw, tg, n=512, src=None, src_keys=None):
                pst, kp = pj.next()
                for k in range(8):
                    if src is None:
                        rhs = hT[:, k, tg * 512:tg * 512 + n]
                        rk = [('hT', k, tg * 512)]
                    else:
                        rhs = src[:, k, 0:n]
                        rk = [src_keys[k]]
                    S.op('pe', lambda e, pst=pst, w=w, k=k, rhs=rhs: e.matmul(pst[:, 0:n], lhsT=w[:, k, 0:128], rhs=rhs,
                                                                             start=(k == 0), stop=(k == 7)),
                         reads=[kw] + rk, writes=[kp], signal=(k == 7))
                return pst, kp

            SS = lambda col: m_ps[:, 64 + col:64 + col + 2]

            def qk_stage(c):
                pq = c % 2
                qrot, qpl, krot, kcT = qrot2[pq], qpl2[pq], krot2[pq], kcT2[pq]
                a68, rstd68, rstd_b = a68_2[pq], rstd68_2[pq], rstd_b2[pq]
                ka68, kr68, krb_ = ('a68', 0), ('rstd68', 0), ('rstd_b', 0)
                wq, kwq = load_wsm(c * 128)
                wk, kwk = load_wsm(512 + c * 128)
                def q_front(ti, tg):
                    w, kw, dstbuf, dkey, gap = ((wq, kwq, qrot, 'qrot', gq_s[:, 0:1]), (wk, kwk, krot, 'krot', qkg[:, 1:2]))[ti]
                    pst, kp = proj_fm(w, kw, tg)
                    sq, ksq = sq_ring.next()
                    S.op('act', lambda e: e.activation(out=sq[:], in_=pst[:], func=AF.Square), reads=[kp], writes=[ksq])
                    if ti == 0:
                        qg, kqg = qpl[:, tg * 512:(tg + 1) * 512], ('qpl', pq, tg)
                    else:
                        qg_t, kqg = qg_ring.next()
                        qg = qg_t[:]
                    if QEV == 'dve':
                        S.op('dve', lambda e: e.tensor_scalar(out=qg, in0=pst[:], scalar1=gap, scalar2=None, op0=ALU.mult),
                             reads=[kp, 'gq_s', 'qkg'], writes=[kqg])
                    else:
                        S.op('act', lambda e: e.activation(out=qg, in_=pst[:], func=AF.Identity, scale=gap),
                             reads=[kp, 'gq_s', 'qkg'], writes=[kqg])
                    return ti, tg, sq, ksq, qg, kqg, dstbuf, dkey

                def q_back(ti, tg, sq, ksq, qg, kqg, dstbuf, dkey):
                    for tb in range(4):
                        col = (ti * 16 + tg * 4 + tb) * 2
                        S.op('pe', lambda e: e.matmul(SS(col), lhsT=sq[:, tb * 128:(tb + 1) * 128], rhs=bo2_b[:], start=True, stop=True),
                             reads=[ksq, 'bo2_b'], writes=[('ss', col), 'mbank'], signal=(tb == 3))
                    rq, krq = axr.next()
                    S.op('pe', lambda e: e.matmul(rq[:], lhsT=perm_b[:], rhs=qg, start=True, stop=True),
                         reads=[kqg, 'perm_b'], writes=[krq])
                    t1, kt1 = t1_ring.next()
                    t2, kt2 = t2_ring.next()
                    S.op('dve', lambda e: e.tensor_tensor(out=t1[:], in0=qg, in1=cos_b[:, tg * 512:(tg + 1) * 512], op=ALU.mult),
                         reads=[kqg, 'cos_b'], writes=[kt1])
                    S.op('dve', lambda e: e.tensor_tensor(out=t2[:], in0=rq[:], in1=sin_b[:, tg * 512:(tg + 1) * 512], op=ALU.mult),
                         reads=[krq, 'sin_b'], writes=[kt2])
                    S.op('dve', lambda e: e.tensor_tensor(out=dstbuf[:, tg * 512:(tg + 1) * 512], in0=t1[:], in1=t2[:], op=ALU.add),
                         reads=[kt1, kt2], writes=[(dkey, pq, tg)])

                pend = None
                for ti in range(2):
                    for tg in range(4):
                        cur = q_front(ti, tg)
                        if pend is not None:
                            q_back(*pend)
                        pend = cur
                        yield
                q_back(*pend)
                pst, kp = proj_fm(wk, kwk, 0, n=NCTX, src=hcT, src_keys=[('hcT', k, 0) for k in range(8)])
                sq, ksq = sq_ring.next()
                S.op('act', lambda e: e.activation(out=sq[:, 0:NCTX], in_=pst[:, 0:NCTX], func=AF.Square), reads=[kp], writes=[ksq])
                S.op('act', lambda e: e.activation(out=kcT[:], in_=pst[:, 0:NCTX], func=AF.Identity, scale=qkg[:, 1:2]),
                     reads=[kp, 'qkg'], writes=[('kcT', pq)])
                for tb in range(2):
                    col = 64 + tb * 2
                    S.op('pe', lambda e: e.matmul(SS(col), lhsT=sq[:, tb * 128:(tb + 1) * 128], rhs=bo2_b[:], start=True, stop=True),
                         reads=[ksq, 'bo2_b'], writes=[('ss', col), 'mbank'], signal=(tb == 1))
                sskeys = [('ss', col) for col in range(0, 68, 2)]
                S.op('dve', lambda e: e.tensor_scalar(out=a68[:], in0=m_ps[:, 64:132], scalar1=1.0 / 64, scalar2=EPS,
                                                      op0=ALU.mult, op1=ALU.add), reads=sskeys + ['mbank'], writes=[ka68])
                yield
                rsqrt_small(a68[:], rstd68[:], 68, [ka68], [kr68], iters=4, tag=pq)
                for _ in range(8):
                    yield
                S.op('dve', lambda e: e.tensor_copy(out=rstd_b[:], in_=rstd68[:]), reads=[kr68], writes=[krb_])
                yield

                def make_rep(blk0, nblk):
                    rep, krep = rep_ring.next()
                    S.op('pool', lambda e: e.tensor_copy(
                        out=rep[:, 0:nblk, :].rearrange("p b (h d) -> p b h d", h=2),
                        in_=rstd_b[:, 2 * blk0:2 * (blk0 + nblk)].rearrange("p (b h) -> p b h", h=2).unsqueeze(3).to_broadcast([128, nblk, 2, 64])),
                        reads=[krb_], writes=[krep])
                    return rep, krep, nblk

                def bcast(rep, krep, nblk):
                    rb, krb = pj.next()
                    for tb in range(nblk):
                        S.op('pe', lambda e: e.matmul(rb[:, tb * 128:(tb + 1) * 128], lhsT=rep[:, tb, :], rhs=ident_b[:], start=True, stop=True),
                             reads=[krep, 'ident_b'], writes=[krb], signal=(tb == nblk - 1))
                    return rb, krb

                steps = [(ti, tg) for ti in range(2) for tg in range(4)] + [(2, 0)]
                blk_of = lambda st: (st[0] * 16 + st[1] * 4, 4) if st[0] < 2 else (32, 2)
                reps = [make_rep(*blk_of(steps[0]))]
                for n_, (ti, tg) in enumerate(steps):
                    if n_ + 1 < len(steps):
                        reps.append(make_rep(*blk_of(steps[n_ + 1])))
                    rb, krb = bcast(*reps[n_])
                    if ti < 2:
                        dstbuf, dkey = ((qrot, 'qrot'), (krot, 'krot'))[ti]
                        S.op('dve', lambda e: e.tensor_tensor(out=dstbuf[:, tg * 512:(tg + 1) * 512], in0=dstbuf[:, tg * 512:(tg + 1) * 512],
                                                              in1=rb[:], op=ALU.mult),
                             reads=[krb, (dkey, pq, tg)], writes=[(dkey, pq, tg)])
                        if ti == 0:
                            S.op('dve', lambda e: e.tensor_tensor(out=qpl[:, tg * 512:(tg + 1) * 512], in0=qpl[:, tg * 512:(tg + 1) * 512],
                                                                  in1=rb[:], op=ALU.mult),
                                 reads=[krb, ('qpl', pq, tg)], writes=[('qpl', pq, tg)])
                    else:
                        S.op('dve', lambda e: e.tensor_tensor(out=kcT[:], in0=kcT[:], in1=rb[:, 0:NCTX], op=ALU.mult),
                             reads=[krb, ('kcT', pq)], writes=[('kcT', pq)])
                    yield

            def za_stage(c):
                pq = c % 2
                sg = sg2[pq]
                w, kw = load_wsm(1536 + c * 128)
                for tg in range(4):
                    pst, kp = proj_fm(w, kw, tg)
                    th, kth = th_ring.next()
                    S.op('act', lambda e: e.activation(out=th[:], in_=pst[:], func=AF.Tanh, scale=0.5), reads=[kp], writes=[kth])
                    S.op('dve', lambda e: e.scalar_tensor_tensor(out=sg[:, tg * 512:(tg + 1) * 512], in0=th[:], scalar=1.0, in1=pst[:],
                                                                 op0=ALU.add, op1=ALU.mult), reads=[kth, kp], writes=[('sg', pq, tg)])
                    yield

            def attn_stage(c):
                pq = c % 2
                qrot, qpl, krot, kcT, sg = qrot2[pq], qpl2[pq], krot2[pq], kcT2[pq], sg2[pq]
                for hl_ in range(2):
                    S.dma('pool', bt2[:, hl_, :, :].rearrange("p b c -> p (b c)"), Dm['biasBT'][2 * c + hl_, :, :],
                          writes=[('bt', hl_)], slot=f'bt{hl_}')
                ctx_ready = {}
                items = [(tg_, hl_) for tg_ in range(4) for hl_ in range(2)]

                def emit_ctx(tg_, hl_):
                    hp_ = slice(hl_ * 64, (hl_ + 1) * 64)
                    Pc_, kPc_ = Pc_ring.next()
                    for ct in range(2):
                        sps, ksp = sr.next()
                        S.op('pe', lambda e: e.matmul(sps[:], lhsT=kcT[hp_, ct * 128:(ct + 1) * 128], rhs=qpl[hp_, tg_ * 512:(tg_ + 1) * 512],
                                                      start=True, stop=True), reads=[('kcT', pq), ('qpl', pq, tg_)], writes=[ksp])
                        S.op('act', lambda e: e.activation(out=Pc_[:, ct, :], in_=sps[:], func=AF.Exp), reads=[ksp], writes=[(kPc_, ct)])
                    ctx_ready[(tg_, hl_)] = (Pc_, kPc_)

                emit_ctx(0, 0)
                for tg in range(4):
                    plan = tg_plan(tg)
                    banks, cur, used = [], [], 0
                    for ent in plan:
                        w_ = ent[2] * 64
                        if used + w_ > 512:
                            banks.append(cur)
                            cur, used = [], 0
                        cur.append((ent, used))
                        used += w_
                    banks.append(cur)
                    for hl in range(2):
                        h = 2 * c + hl
                        hp = slice(hl * 64, (hl + 1) * 64)
                        BK = o_ps if hl == 0 else den_ps
                        kO = ('OT', hl)
                        Pc, kPc = ctx_ready.pop((tg, hl))

                        def emit_scores(bank):
                            sps, ksp = sr.next()
                            ncol = 0
                            for (j, i0, n, segs), off in bank:
                                S.op('pe', lambda e: e.matmul(sps[:, off:off + n * 64], lhsT=krot[hp, j * 128:(j + 1) * 128],
                                                              rhs=qrot[hp, i0 * 64:(i0 + n) * 64], start=True, stop=False),
                                     reads=[('krot', pq, j // 4), ('qrot', pq, tg)], writes=[ksp], signal=False)
                                for si, (r0, nr, sl) in enumerate(segs):
                                    o2 = off + (r0 - i0) * 64
                                    S.op('pe', lambda e: e.matmul(sps[:, o2:o2 + nr * 64], lhsT=ident_b[:],
                                                                  rhs=bt2[:, hl, sl:sl + nr, :].rearrange("p a b -> p (a b)"),
                                                                  start=False, stop=(si == len(segs) - 1)),
                                         reads=[('bt', hl), 'ident_b'], writes=[ksp], signal=(si == len(segs) - 1))
                                ncol = off + n * 64
                            Pl, kPl = Pl_ring.next()
                            S.op('act', lambda e: e.activation(out=Pl[:, 0:ncol], in_=sps[:, 0:ncol], func=AF.Exp), reads=[ksp], writes=[kPl])
                            return Pl, kPl

                        def emit_pv_ctx():
                            for ct in range(2):
                                S.op('pe', lambda e: e.matmul(BK[:], lhsT=Vc_aug[:, ct, c, hl:hl + 2, :].rearrange("p a b -> p (a b)"),
                                                              rhs=Pc[:, ct, :], start=(ct == 0), stop=False),
                                     reads=[(kPc, ct), ('Vc', ct), 'Vc_ones'], writes=[kO], signal=False)

                        def emit_pv(bank, Pl, kPl, lastbank):
                            for bi, ((j, i0, n, segs), off) in enumerate(bank):
                                cs_ = slice((i0 - 8 * tg) * 64, (i0 - 8 * tg + n) * 64)
                                fin = lastbank and bi == len(bank) - 1
                                S.op('pe', lambda e: e.matmul(BK[:, cs_], lhsT=V_aug[:, j, c, hl:hl + 2, :].rearrange("p a b -> p (a b)"),
                                                              rhs=Pl[:, off:off + n * 64], start=False, stop=fin),
                                     reads=[kPl, ('V', j), 'V_ones'], writes=[kO], signal=(bi == len(bank) - 1))

                        sc = [emit_scores(banks[0])]
                        if len(banks) > 1:
                            sc.append(emit_scores(banks[1]))
                        emit_pv_ctx()
                        yield
                        nxt_item = items.index((tg, hl)) + 1
                        for bi in range(len(banks)):
                            emit_pv(banks[bi], sc[bi][0], sc[bi][1], bi == len(banks) - 1)
                            if bi + 2 < len(banks):
                                sc.append(emit_scores(banks[bi + 2]))
                            if bi == min(1, len(banks) - 1) and nxt_item < len(items):
                                emit_ctx(*items[nxt_item])
                            yield
                    S.op('act', lambda e: e.activation(out=oS[0:64, :], in_=o_ps[0:64, :], func=AF.Identity), reads=[('OT', 0)], writes=[('oS', 0)])
                    S.op('act', lambda e: e.activation(out=rD[0:64, :], in_=o_ps[64:128, :], func=AF.Identity), reads=[('OT', 0)], writes=[('rD', 0)])
                    S.op('dve', lambda e: e.tensor_copy(out=rD[64:128, :], in_=den_ps[0:64, :]), reads=[('OT', 1)], writes=[('rD', 1)])
                    S.op('dve', lambda e: e.tensor_copy(out=oS[64:128, :], in_=den_ps[64:128, :]), reads=[('OT', 1)], writes=[('oS', 1)])
                    S.op('dve', lambda e: e.reciprocal(out=rD[:], in_=rD[:]), reads=[('rD', 0), ('rD', 1)], writes=[('rD', 0), ('rD', 1)])
                    S.op('dve', lambda e: e.scalar_tensor_tensor(out=gD[:], in0=rD[:], scalar=0.5, in1=sg[:, tg * 512:(tg + 1) * 512],
                                                                 op0=ALU.mult, op1=ALU.mult), reads=[('rD', 0), ('rD', 1), ('sg', pq, tg)], writes=['gD'])
                    S.op('dve', lambda e: e.tensor_tensor(out=yT[:, c, tg * 512:(tg + 1) * 512], in0=oS[:], in1=gD[:], op=ALU.mult),
                         reads=[('oS', 0), ('oS', 1), 'gD'], writes=[('yT', c, tg)])
                    attn_prog['c'], attn_prog['tg'] = c, tg
                    yield

            conv_cols = []
            for cc_ in range(4):
                conv_cols += [3072 + cc_ * 128, 2048 + cc_ * 128, 3584 + cc_ * 128, 2560 + cc_ * 128]
            conv_issued = {}

            def conv_piece(i):
                if i >= len(conv_cols):
                    return None
                if i not in conv_issued:
                    conv_issued[i] = load_wsm(conv_cols[i], wsmc_ring)
                return conv_issued[i]

            def conv_stage(cc):
                base = 4 * cc
                wcg, kwcg = conv_piece(base)
                wu, kwu = conv_piece(base + 1)
                conv_piece(base + 2)
                for tg in range(4):
                    pcg, kpcg = proj_fm(wcg, kwcg, tg)
                    cg, kcg = cg_ring.next()
                    S.op('act', lambda e: e.activation(out=cg[:], in_=pcg[:], func=AF.Identity), reads=[kpcg], writes=[kcg])
                    yield
                    pu, kpu = proj_fm(wu, kwu, tg)
                    S.op('dve', lambda e: e.tensor_tensor(out=a_buf[:, 1 + tg * 512:1 + (tg + 1) * 512], in0=pu[:], in1=cg[:], op=ALU.mult),
                         reads=[kpu, kcg], writes=[('a', tg)])
                    yield
                wzc, kwzc = conv_piece(base + 2)
                wbg, kwbg = conv_piece(base + 3)
                for tg in range(4):
                    pz, kpz = proj_fm(wzc, kwzc, tg)
                    th, kth = th_ring.next()
                    S.op('act', lambda e: e.activation(out=th[:], in_=pz[:], func=AF.Tanh, scale=0.5), reads=[kpz], writes=[kth])
                    tz, ktz = tz_ring.next()
                    S.op('dve', lambda e: e.scalar_tensor_tensor(out=tz[:], in0=th[:], scalar=1.0, in1=pz[:], op0=ALU.add, op1=ALU.mult),
                         reads=[kth, kpz], writes=[ktz])
                    yield
                    pb_, kpb = proj_fm(wbg, kwbg, tg)
                    S.op('dve', lambda e: e.scalar_tensor_tensor(out=gz[:, tg * 512:(tg + 1) * 512], in0=pb_[:], scalar=0.5, in1=tz[:],
                                                                 op0=ALU.mult, op1=ALU.mult), reads=[kpb, ktz], writes=[('gz', tg)])
                    yield
                conv_piece(base + 4)
                conv_piece(base + 5)
                for tg in range(4):
                    pc_, kpc = axr.next()
                    rk = [('a', t) for t in range(max(0, tg - 1), min(4, tg + 2))] + ['a_pad']
                    for i in range(3):
                        S.op('pe', lambda e: e.matmul(pc_[:], lhsT=diag[:, cc * 3 + i, :], rhs=a_buf[:, tg * 512 + i:tg * 512 + i + 512],
                                                      start=(i == 0), stop=(i == 2)),
                             reads=rk + [('diag', cc * 3 + i)], writes=[kpc], signal=(i == 2))
                    S.op('dve', lambda e: e.scalar_tensor_tensor(out=yT[:, 4 + cc, tg * 512:(tg + 1) * 512], in0=pc_[:],
                                                                 scalar=convb[:, cc:cc + 1], in1=gz[:, tg * 512:(tg + 1) * 512],
                                                                 op0=ALU.add, op1=ALU.mult),
                         reads=[kpc, 'convb', ('gz', tg)], writes=[('yT', 4 + cc, tg)])
                    yield

            ep_ring = Ring('ep', [f32view(hT, k) for k in range(8)])
            xbufs = {}
            epi_state = {'next_tt': 0}
            attn_prog = {'c': -1, 'tg': -1}

            def xload(tt):
                xr, kxr = ep_ring.next()
                kx = [('hT', kxr[1], o) for o in range(0, S_TOK, 512)]
                S.dma('pool', xr, Dm['x'][tt * 128:(tt + 1) * 128, :], writes=kx, slot=f'ep{kxr[1]}')
                xbufs[tt] = (xr, kxr, kx)

            def epi_stage():
                for tt in range(8):
                    xload(tt)
                yield
                for tt in range(16):
                    tg = tt // 4
                    epi_state['next_tt'] = tt
                    xr, kxr, kx = xbufs[tt]
                    for half in range(2):
                        w, kw = wo[half]
                        pst, kp = pj.next()
                        for k in range(8):
                            S.op('pe', lambda e: e.matmul(pst[:], lhsT=yT[:, k, tt * 128:(tt + 1) * 128], rhs=w[:, k, :],
                                                          start=(k == 0), stop=(k == 7)),
                                 reads=[kw, ('yT', k, tg)], writes=[kp], signal=(k == 7))
                        S.op('dve', lambda e: e.tensor_tensor(out=xr[:, half * 512:(half + 1) * 512], in0=pst[:],
                                                              in1=xr[:, half * 512:(half + 1) * 512], op=ALU.add),
                             reads=[kp] + kx, writes=kx)
                        if half == 1:
                            out_toks.append(S.dma('sp', out[tt * 128:(tt + 1) * 128, :], xr, reads=kx, slot=f'o{kxr[1]}'))
                            if tt + 8 < 16:
                                xload(tt + 8)
                        epi_state['next_tt'] = tt + (1 if half == 1 else 0)
                        yield

            def run(gen):
                for _ in gen:
                    pass

            def chain(*gens):
                for g in gens:
                    yield from g

            def interleave(ga, gb, na=1, nb=1):
                ga, gb = iter(ga), iter(gb)
                da = db = False
                while not (da and db):
                    for _ in range(na):
                        if not da:
                            try:
                                next(ga)
                            except StopIteration:
                                da = True
                    for _ in range(nb):
                        if not db:
                            try:
                                next(gb)
                            except StopIteration:
                                db = True

            if lvl >= 5 and npairs == 4 and INTERLEAVE:
                Dq = [('gate_dma', gate_dma_stage())]
                for c in range(1, 4):
                    Dq += [(f'qk{c}', qk_stage(c)), (f'za{c}', za_stage(c))]
                    if c == 1:
                        Dq.append(('gate_mm', gate_mm_stage()))
                    if c == 2:
                        Dq.append(('wout_fold', wout_fold_stage()))
                Fq = [(f'conv{c}', conv_stage(c)) for c in range(4)]
                di, fi = [0], [0]

                def step(q, idx):
                    while idx[0] < len(q):
                        try:
                            next(q[idx[0]][1])
                            return True
                        except StopIteration:
                            idx[0] += 1
                    return False

                def drain_until(label):
                    i_ = [l for l, _ in Dq].index(label)
                    while di[0] <= i_:
                        if not step(Dq, di):
                            break

                startB = chain(qk_stage(0), za_stage(0))
                bdone = [False]

                def stepB():
                    if not bdone[0]:
                        try:
                            next(startB)
                        except StopIteration:
                            bdone[0] = True
                    return not bdone[0]

                npro = 0
                for _ in PRO:
                    npro += 1
                    if npro > PRO_LAG and npro % 2 == 0:
                        step(Fq, fi)
                while stepB():
                    step(Fq, fi)
                EPI = epi_stage() if lvl >= 6 else iter(())
                epi_done = [False]

                def step_epi():
                    if epi_done[0] or fi[0] < len(Fq) or di[0] < len(Dq):
                        return False
                    if not (attn_prog['c'] == 3 and epi_state['next_tt'] // 4 <= attn_prog['tg']):
                        return False
                    try:
                        next(EPI)
                    except StopIteration:
                        epi_done[0] = True
                        return False
                    return True

                accd = accf = 0.0
                for c in range(4):
                    if c >= 1:
                        drain_until(f'za{c}')
                    rd_, rf_ = (RD, RF0) if c < 3 else (1.0, RF1)
                    for _ in attn_stage(c):
                        accd += rd_
                        accf += rf_
                        while accd >= 1.0:
                            accd -= 1.0
                            step(Dq, di)
                        while accf >= 1.0:
                            accf -= 1.0
                            if not step(Fq, fi) and c == 3:
                                step_epi()
                while step(Dq, di):
                    pass
                while step(Fq, fi):
                    pass
                for _ in EPI:
                    pass
            else:
                run(gate_dma_stage())
                run(gate_mm_stage())
                run(wout_fold_stage())
                for c in range(npairs):
                    if lvl >= 2:
                        run(qk_stage(c))
                    if lvl >= 3:
                        run(za_stage(c))
                    if lvl >= 4:
                        run(attn_stage(c))
                    if lvl >= 5:
                        run(conv_stage(c))
                if lvl >= 6:
                    run(epi_stage())

            def dump(name, ap, shape, keys, dt=BF16):
                t = nc.dram_tensor('dbg_' + name, shape, dt, kind="ExternalOutput").ap()
                dbg_out[name] = t
                return S.dma('sp', t, ap, reads=keys, slot=slot())
            if 'hT' in debug:
                dbg_toks.append(dump('hT', hT[:], [128, 8, S_TOK], HT_ALL))
                dbg_toks.append(dump('hcT', hcT[:], [128, 8, NCTX], HC_ALL))
                dbg_toks.append(dump('V', V_aug[:], [128, 16, 4, 3, 64], [('V', t) for t in range(16)] + ['V_ones']))
                dbg_toks.append(dump('qrot', qrot2[1][:], [128, S_TOK], [('qrot', 1, t) for t in range(4)]))
                dbg_toks.append(dump('qpl', qpl2[1][:], [128, S_TOK], [('qpl', 1, t) for t in range(4)]))
                dbg_toks.append(dump('krot', krot2[1][:], [128, S_TOK], [('krot', 1, t) for t in range(4)]))
                dbg_toks.append(dump('kcT', kcT2[1][:], [128, NCTX], [('kcT', 1)]))
                dbg_toks.append(dump('yT', yT[:], [128, 8, S_TOK], [('yT', k, t) for k in range(8) for t in range(4)]))
                dbg_toks.append(dump('Gbc', Gbc[:], [128, D], [('Gbc', 0), ('Gbc', 1)]))
                dbg_toks.append(dump('mod', mod[:], [128, 24, 2], ['mod_ss', 'mod_g'], dt=F32))

        try:
            body()
        except _Cut:
            pass
        S.wait_tokens('sp', out_toks + dbg_toks)

        with nc.allow_low_precision(reason="bf16 1/den before the PE half-swap"), nc.Block() as block:
            @block.tensor
            def _(eng):
                for f in S.prog['pe']:
                    f(eng)

            @block.scalar
            def _(eng):
                for f in S.prog['act']:
                    f(eng)

            @block.vector
            def _(eng):
                for f in S.prog['dve']:
                    f(eng)

            @block.gpsimd
            def _(eng):
                for f in S.prog['pool']:
                    f(eng)

            @block.sync
            def _(eng):
                for f in S.prog['sp']:
                    f(eng)
    return nc, dbg_out


def _consts():
    ident = np.eye(128, dtype=np.float32)
    perm = np.zeros((128, 128), np.float32)
    for d in range(128):
        dl = d % 64
        axis, half, f = dl // 32, (dl % 32) // 16, dl % 16
        sw = (d // 64) * 64 + axis * 32 + (1 - half) * 16 + f
        perm[sw, d] = 1.0
    swap = np.zeros((128, 128), np.float32)
    for p in range(128):
        swap[(p + 64) % 128, p] = 1.0
    bo2 = np.zeros((128, 2), np.float32)
    bo2[:64, 0] = 1.0
    bo2[64:, 1] = 1.0
    nf = 16
    inv = (np.float32(10000.0) ** (-np.arange(nf, dtype=np.float32) / np.float32(nf))).astype(np.float32)
    pos = np.arange(S_TOK, dtype=np.int32)
    row = (pos // 64).astype(np.float32)
    col = (pos % 64).astype(np.float32)
    cos = np.zeros((128, S_TOK), np.float32)
    sin = np.zeros((128, S_TOK), np.float32)
    for p in range(128):
        dl = p % 64
        axis, half, f = dl // 32, (dl % 32) // 16, dl % 16
        ang = ((row if axis == 0 else col) * inv[f]).astype(np.float32)
        cos[p] = np.cos(ang)
        sin[p] = np.sin(ang) * (-1.0 if half == 0 else 1.0)
    return ident, perm, bo2, cos, sin, swap


def _bias_table(rpb):
    qc = np.arange(64)[None, :]
    kc = np.arange(64)[:, None]
    cs = np.clip(qc - 8, 0, 48)
    valid = (kc >= cs) & (kc < cs + 16)
    dc = np.clip(kc - qc + 15, 0, 30)
    BT = np.full((8, 2, 64, 28, 64), NEG, np.float32)
    for var, (base, d0, n) in _SLOT0.items():
        for s_ in range(n):
            d = d0 + s_
            for rloc in range(2):
                dr = 7 - d + rloc
                if var == 'int':
                    inband = (d <= 4) if rloc == 0 else (d >= -2)
                else:
                    inband = True
                if not inband or dr < 0 or dr > 14:
                    continue
                vals = rpb[:, dr][:, dc]
                BT[:, rloc, :, base + s_, :] = np.where(valid[None], vals, np.float32(NEG))
    return np.ascontiguousarray(BT.reshape(8, 128, 28 * 64))


_CACHE = {}


def kernel(x, c, ctx, c_ctx, w_ada, b_ada, norm_g, w_in, q_norm_g, k_norm_g, rpb, conv_w, conv_b, w_out, _debug=(), _upto='all', _npairs=4):
    x = np.asarray(x, np.float32); c = np.asarray(c, np.float32); ctx = np.asarray(ctx, np.float32)
    key = (tuple(_debug), _upto, _npairs)
    if key not in _CACHE:
        _CACHE[key] = build_program(debug=_debug, upto=_upto, npairs=_npairs)
    nc, dbg = _CACHE[key]
    ident, perm, bo2, cos, sin, swap = _consts()
    lay = lambda v, n: np.ascontiguousarray(np.asarray(v, np.float32).reshape(n, 128).T)
    shared = {
        'w_ada': np.ascontiguousarray(np.asarray(w_ada, np.float32)[0]),
        'w_in': np.ascontiguousarray(np.asarray(w_in, np.float32)[0]),
        'w_out': np.ascontiguousarray(np.asarray(w_out, np.float32)[0]),
        'bada': lay(b_ada[0], 24), 'normg': lay(norm_g[0], 8),
        'qkg': np.ascontiguousarray(np.stack([np.tile(np.asarray(q_norm_g, np.float32)[0], 2),
                                              np.tile(np.asarray(k_norm_g, np.float32)[0], 2)], axis=1)),
        'biasBT': _bias_table(np.asarray(rpb, np.float32)[0]),
        'convw': np.ascontiguousarray(np.asarray(conv_w, np.float32)[0].reshape(3, 4, 128).transpose(2, 1, 0).reshape(128, 12)),
        'convb': lay(conv_b[0], 4),
        'ident': ident, 'perm': perm, 'bo2': bo2, 'cos': cos, 'sin': sin, 'swap': swap,
    }
    cc = lay(c_ctx, 8)
    in_maps = []
    for b in range(8):
        cv = np.stack([lay(c[b], 8), cc], axis=2).reshape(128, 16)
        m = dict(shared)
        m['x'] = np.ascontiguousarray(x[b]); m['ctx'] = np.ascontiguousarray(ctx[b]); m['cvec'] = np.ascontiguousarray(cv)
        in_maps.append(m)
    res = run_bass_kernel_spmd(nc, in_maps, core_ids=list(range(8)))
    outp = np.stack([np.asarray(r['out'], np.float32) for r in res.results], axis=0)
    if _debug:
        return outp, [r for r in res.results]
    return outp
```
